# Optimizing a Trainium2 kernel written in Bass

```python
import math
import jax, jax.numpy as jnp
from jax import lax
import numpy as np

D_MODEL = 1024
BATCH = 2
SEQ = 8192
DEPTH = 1
DEC_BATCH = 128
DEC_SEQ = 4
PAST_LEN = 2048
PAGE_SIZE = 128

MIX_W = D_MODEL
MLSTM_W = MIX_W // 2
MLSTM_HEADS = 4
MLSTM_DH = MLSTM_W // MLSTM_HEADS
ATTN_W = MIX_W - MLSTM_W
ATTN_HEADS = 8
ATTN_DH = ATTN_W // ATTN_HEADS
DILATED = ((128, 1), (512, 4), (2048, 16))
WINDOW_MAX = 2048
BLK = 128
CONV_W = 4
CHUNK = 64
MEM_LEN = 256
MEM_HEADS = 4
MEM_DH = D_MODEL // MEM_HEADS
D_FF = -(-8 * D_MODEL // (3 * 256)) * 256
EPS = 1e-6
_WIDTHS = (MLSTM_W, MLSTM_W, MLSTM_W, MLSTM_W, MLSTM_HEADS, MLSTM_HEADS, ATTN_W, ATTN_W, ATTN_W)
SPLIT_POINTS = tuple(int(s) for s in np.cumsum(_WIDTHS)[:-1])
N_IN = int(sum(_WIDTHS))

kernel_name = 'hymba_mlstm_dilated_swa_memxattn_step'


def rmsnorm(x, g):
    xf = x.astype(jnp.float32)
    y = xf * lax.rsqrt(jnp.mean(xf * xf, axis=-1, keepdims=True) + EPS)
    return (y * g.astype(jnp.float32)).astype(x.dtype)


def headwise_layernorm(h, g):
    mu = jnp.mean(h, axis=-1, keepdims=True)
    var = jnp.mean(jnp.square(h - mu), axis=-1, keepdims=True)
    return (h - mu) * lax.rsqrt(var + EPS) * g.astype(jnp.float32).reshape(MLSTM_HEADS, MLSTM_DH)


def softmax_lse(s):
    mx = jnp.max(s, axis=-1, keepdims=True)
    e = jnp.exp(s - mx)
    den = jnp.sum(e, axis=-1, keepdims=True)
    return e / den, (mx + jnp.log(den))[..., 0]


def causal_conv(u, buf, w, b):
    T = u.shape[1]
    up = jnp.concatenate([buf.astype(u.dtype), u], axis=1)
    y = b
    for j in range(CONV_W):
        y = y + up[:, j:j + T] * w[j]
    return y, up[:, -(CONV_W - 1):]


def mlstm_chunkwise(q, k, v, i_pre, logf, C0, n0, m0):
    B, T, H, E = q.shape
    L = math.gcd(T, CHUNK)
    nc = T // L
    causal = jnp.tril(jnp.ones((L, L), dtype=bool))

    def chunks(a):
        return jnp.moveaxis(a.reshape((B, nc, L) + a.shape[2:]), 1, 0)

    def step(carry, inp):
        C, n, m = carry
        qc, kc, vc, ic, fc = inp
        b = jnp.cumsum(fc, axis=1)
        a = b + m[:, None, :]
        dmat = b[:, :, None, :] - b[:, None, :, :] + ic[:, None, :, :]
        dmat = jnp.where(causal[None, :, :, None], dmat, -jnp.inf)
        mt = jnp.maximum(a, jnp.max(dmat, axis=2))
        wa = jnp.exp(a - mt)
        sc = jnp.einsum('bthe,bshe->btsh', qc, kc) * jnp.exp(dmat - mt[:, :, None, :])
        num = wa[..., None] * jnp.einsum('bthd,bhde->bthe', qc, C) + jnp.einsum('btsh,bshe->bthe', sc, vc)
        den = wa * jnp.einsum('bthd,bhd->bth', qc, n) + jnp.sum(sc, axis=2)
        h = num / jnp.maximum(jnp.abs(den), jnp.exp(-mt))[..., None]
        bl = b[:, -1]
        g = bl[:, None, :] - b + ic
        m_new = jnp.maximum(bl + m, jnp.max(g, axis=1))
        wc = jnp.exp(bl + m - m_new)
        ws = jnp.exp(g - m_new[:, None, :])
        C_new = wc[..., None, None] * C + jnp.einsum('bsh,bshd,bshe->bhde', ws, kc, vc)
        n_new = wc[..., None] * n + jnp.einsum('bsh,bshd->bhd', ws, kc)
        return (C_new, n_new, m_new), h

    (C, n, m), hs = lax.scan(step, (C0, n0, m0), tuple(chunks(a) for a in (q, k, v, i_pre, logf)))
    h = jnp.moveaxis(hs, 0, 1).reshape(B, T, H, E)
    return h, C, n, m


def combine_branches(outs, lses):
    w = jax.nn.softmax(jnp.stack(lses, axis=0), axis=0)
    return jnp.einsum('nbth,nbthe->bthe', w, jnp.stack(outs, axis=0))


def to_residue_blocks(a, d, n, npad, front):
    B, T, H, E = a.shape
    a = a.reshape(B, n, d, H, E).transpose(0, 2, 1, 3, 4)
    return jnp.pad(a, ((0, 0), (0, 0), (front, npad - n), (0, 0), (0, 0)))


def dilated_attention_prompt(q, k, v):
    B, T, H, E = q.shape
    scale = E ** -0.5
    outs, lses = [], []
    for window, d in DILATED:
        rel = window // d
        n = T // d
        nb = -(-n // BLK)
        npad = nb * BLK
        qs = to_residue_blocks(q, d, n, npad, 0).reshape(B, d, nb, BLK, H, E)
        ks = to_residue_blocks(k, d, n, npad, BLK).reshape(B, d, nb + 1, BLK, H, E)
        vs = to_residue_blocks(v, d, n, npad, BLK).reshape(B, d, nb + 1, BLK, H, E)
        kb = jnp.concatenate([ks[:, :, :-1], ks[:, :, 1:]], axis=3)
        vb = jnp.concatenate([vs[:, :, :-1], vs[:, :, 1:]], axis=3)
        s = jnp.einsum('bdnqhe,bdnkhe->bdnhqk', qs, kb) * scale
        qi = jnp.arange(BLK)[:, None]
        kj = jnp.arange(2 * BLK)[None, :]
        dist = qi + BLK - kj
        kpos = jnp.arange(nb)[:, None, None] * BLK + kj[None] - BLK
        valid = ((dist >= 0) & (dist <= rel))[None] & (kpos >= 0)
        s = jnp.where(valid[None, None, :, None], s, -jnp.inf)
        p, lse = softmax_lse(s)
        o = jnp.einsum('bdnhqk,bdnkhe->bdnqhe', p, vb).reshape(B, d, npad, H, E)[:, :, :n]
        outs.append(o.transpose(0, 2, 1, 3, 4).reshape(B, T, H, E))
        lse = lse.transpose(0, 1, 2, 4, 3).reshape(B, d, npad, H)[:, :, :n]
        lses.append(lse.transpose(0, 2, 1, 3).reshape(B, T, H))
    return combine_branches(outs, lses)


def dilated_attention_step(q, k_new, v_new, k_buf, v_buf):
    B, T, H, E = q.shape
    wb = k_buf.shape[1]
    scale = E ** -0.5
    kk = jnp.concatenate([k_buf.astype(jnp.float32), k_new], axis=1)
    vv = jnp.concatenate([v_buf.astype(jnp.float32), v_new], axis=1)
    t = jnp.arange(T)
    outs, lses = [], []
    for window, d in DILATED:
        r = jnp.arange(window // d + 1)
        idx = wb + t[:, None] - d * r[None, :]
        valid = idx >= 0
        idx = jnp.maximum(idx, 0)
        kg = kk[:, idx]
        vg = vv[:, idx]
        s = jnp.einsum('bthe,btrhe->bthr', q, kg) * scale
        s = jnp.where(valid[None, :, None, :], s, -jnp.inf)
        p, lse = softmax_lse(s)
        outs.append(jnp.einsum('bthr,btrhe->bthe', p, vg))
        lses.append(lse)
    return combine_branches(outs, lses)


def mixing_sublayer(x, conv_buf, C0, n0, m0, k_buf, v_buf, g_mix, w_in, conv_w, conv_b, b_gates, g_mh, w_out):
    B, T, _ = x.shape
    f32 = jnp.float32
    h = rmsnorm(x, g_mix)
    qm, km, vm, om, ig, fg, qa, ka, va = jnp.split(h @ w_in, SPLIT_POINTS, axis=-1)
    qk, new_conv = causal_conv(jnp.concatenate([qm, km], axis=-1), conv_buf, conv_w, conv_b)
    qk = jax.nn.silu(qk).astype(f32)
    qm, km = jnp.split(qk, 2, axis=-1)
    qm = qm.reshape(B, T, MLSTM_HEADS, MLSTM_DH)
    km = km.reshape(B, T, MLSTM_HEADS, MLSTM_DH) * (MLSTM_DH ** -0.5)
    vm = vm.astype(f32).reshape(B, T, MLSTM_HEADS, MLSTM_DH)
    gates = jnp.concatenate([ig, fg], axis=-1).astype(f32) + b_gates.astype(f32)
    i_pre, f_pre = jnp.split(gates, 2, axis=-1)
    logf = jax.nn.log_sigmoid(f_pre)
    hm, C, n, m = mlstm_chunkwise(qm, km, vm, i_pre, logf, C0.astype(f32), n0.astype(f32), m0.astype(f32))
    hm = headwise_layernorm(hm, g_mh).reshape(B, T, MLSTM_W) * jax.nn.sigmoid(om.astype(f32))
    qa = qa.astype(f32).reshape(B, T, ATTN_HEADS, ATTN_DH)
    ka = ka.astype(f32).reshape(B, T, ATTN_HEADS, ATTN_DH)
    va = va.astype(f32).reshape(B, T, ATTN_HEADS, ATTN_DH)
    if k_buf is None:
        ha = dilated_attention_prompt(qa, ka, va)
        keep = min(WINDOW_MAX, T)
        k_rows, v_rows = ka[:, T - keep:], va[:, T - keep:]
    else:
        ha = dilated_attention_step(qa, ka, va, k_buf, v_buf)
        k_rows, v_rows = ka, va
    mix = jnp.concatenate([hm, ha.reshape(B, T, ATTN_W)], axis=-1).astype(x.dtype)
    return x + mix @ w_out, (k_rows, v_rows, new_conv, C, n, m)


def memory_kv(mem, g_mem, w_mk, w_mv):
    B = mem.shape[0]
    h = rmsnorm(mem, g_mem)
    return ((h @ w_mk).reshape(B, MEM_LEN, MEM_HEADS, MEM_DH),
            (h @ w_mv).reshape(B, MEM_LEN, MEM_HEADS, MEM_DH))


def memory_xattn(x, mem_k, mem_v, g_xattn, w_mq, w_mo):
    B, T, _ = x.shape
    q = (rmsnorm(x, g_xattn) @ w_mq).astype(jnp.float32).reshape(B, T, MEM_HEADS, MEM_DH)
    s = jnp.einsum('bthe,bmhe->bhtm', q, mem_k.astype(jnp.float32)) * (MEM_DH ** -0.5)
    p = jax.nn.softmax(s, axis=-1)
    o = jnp.einsum('bhtm,bmhe->bthe', p, mem_v.astype(jnp.float32)).reshape(B, T, D_MODEL)
    return x + o.astype(x.dtype) @ w_mo


def swiglu_ffn(x, g_ffn, w_gate, w_up, w_down):
    h = rmsnorm(x, g_ffn)
    return x + (jax.nn.silu(h @ w_gate) * (h @ w_up)) @ w_down


def decoder_layer(x, mem_k, mem_v, conv_buf, C0, n0, m0, k_buf, v_buf, g_mix, w_in, conv_w, conv_b, b_gates,
                  g_mh, w_out, g_xattn, w_mq, w_mo, g_ffn, w_gate, w_up, w_down):
    x, st = mixing_sublayer(x, conv_buf, C0, n0, m0, k_buf, v_buf, g_mix, w_in, conv_w, conv_b, b_gates, g_mh, w_out)
    x = memory_xattn(x, mem_k, mem_v, g_xattn, w_mq, w_mo)
    x = swiglu_ffn(x, g_ffn, w_gate, w_up, w_down)
    return x, st


def setup_inputs(seed: int = 0) -> dict:
    key = jax.random.key(seed)
    ks = jax.random.split(key, 32)
    nrm = lambda i, shape, s=1.0: jax.random.normal(ks[i], shape, jnp.float32) * s
    wb = min(WINDOW_MAX, PAST_LEN)
    b_gates = jnp.concatenate([nrm(20, (MLSTM_HEADS,), 0.1),
                               jnp.linspace(3.0, 6.0, MLSTM_HEADS) + nrm(21, (MLSTM_HEADS,), 0.1)])
    return {
        'x_prompt': nrm(0, (BATCH, SEQ, D_MODEL)),
        'x_sample': nrm(1, (DEC_BATCH, DEC_SEQ, D_MODEL)),
        'mem_prompt': nrm(2, (BATCH, MEM_LEN, D_MODEL)),
        'cache_attn_k': nrm(3, (DEC_BATCH, wb, ATTN_HEADS, ATTN_DH)),
        'cache_attn_v': nrm(4, (DEC_BATCH, wb, ATTN_HEADS, ATTN_DH)),
        'cache_mem_k': nrm(5, (DEC_BATCH, MEM_LEN, MEM_HEADS, MEM_DH)),
        'cache_mem_v': nrm(6, (DEC_BATCH, MEM_LEN, MEM_HEADS, MEM_DH)),
        'state_conv': nrm(7, (DEC_BATCH, CONV_W - 1, 2 * MLSTM_W)),
        'state_C': nrm(8, (DEC_BATCH, MLSTM_HEADS, MLSTM_DH, MLSTM_DH), 0.1),
        'state_n': nrm(9, (DEC_BATCH, MLSTM_HEADS, MLSTM_DH), 0.1),
        'state_m': nrm(10, (DEC_BATCH, MLSTM_HEADS)),
        'g_mix': 1.0 + nrm(11, (D_MODEL,), 0.01),
        'w_in': nrm(12, (D_MODEL, N_IN), D_MODEL ** -0.5),
        'conv_w': nrm(13, (CONV_W, 2 * MLSTM_W), CONV_W ** -0.5),
        'conv_b': nrm(14, (2 * MLSTM_W,), 0.01),
        'b_gates': b_gates,
        'g_mh': 1.0 + nrm(15, (MLSTM_W,), 0.01),
        'w_out': nrm(16, (MIX_W, D_MODEL), MIX_W ** -0.5),
        'g_mem': 1.0 + nrm(17, (D_MODEL,), 0.01),
        'w_mk': nrm(18, (D_MODEL, D_MODEL), D_MODEL ** -0.5),
        'w_mv': nrm(19, (D_MODEL, D_MODEL), D_MODEL ** -0.5),
        'g_xattn': 1.0 + nrm(22, (D_MODEL,), 0.01),
        'w_mq': nrm(23, (D_MODEL, D_MODEL), D_MODEL ** -0.5),
        'w_mo': nrm(24, (D_MODEL, D_MODEL), D_MODEL ** -0.5),
        'g_ffn': 1.0 + nrm(25, (D_MODEL,), 0.01),
        'w_gate': nrm(26, (D_MODEL, D_FF), D_MODEL ** -0.5),
        'w_up': nrm(27, (D_MODEL, D_FF), D_MODEL ** -0.5),
        'w_down': nrm(28, (D_FF, D_MODEL), D_FF ** -0.5),
        'g_final': 1.0 + nrm(29, (D_MODEL,), 0.01),
    }


def reference(x_prompt, x_sample, mem_prompt, cache_attn_k, cache_attn_v, cache_mem_k, cache_mem_v,
              state_conv, state_C, state_n, state_m, g_mix, w_in, conv_w, conv_b, b_gates, g_mh, w_out,
              g_mem, w_mk, w_mv, g_xattn, w_mq, w_mo, g_ffn, w_gate, w_up, w_down, g_final):
    f32 = jnp.float32
    B = x_prompt.shape[0]
    p_mem_k, p_mem_v = memory_kv(mem_prompt, g_mem, w_mk, w_mv)
    conv0 = jnp.zeros((B, CONV_W - 1, 2 * MLSTM_W), x_prompt.dtype)
    C0 = jnp.zeros((B, MLSTM_HEADS, MLSTM_DH, MLSTM_DH), f32)
    n0 = jnp.zeros((B, MLSTM_HEADS, MLSTM_DH), f32)
    m0 = jnp.zeros((B, MLSTM_HEADS), f32)
    hp = x_prompt
    for _ in range(DEPTH):
        hp, (p_attn_k, p_attn_v, p_conv, p_C, p_n, p_m) = decoder_layer(
            hp, p_mem_k, p_mem_v, conv0, C0, n0, m0, None, None, g_mix, w_in, conv_w, conv_b, b_gates,
            g_mh, w_out, g_xattn, w_mq, w_mo, g_ffn, w_gate, w_up, w_down)
    y_prompt = rmsnorm(hp, g_final)
    hs = x_sample
    for _ in range(DEPTH):
        hs, (s_attn_k, s_attn_v, s_conv, s_C, s_n, s_m) = decoder_layer(
            hs, cache_mem_k, cache_mem_v, state_conv, state_C, state_n, state_m, cache_attn_k, cache_attn_v,
            g_mix, w_in, conv_w, conv_b, b_gates, g_mh, w_out, g_xattn, w_mq, w_mo, g_ffn, w_gate, w_up, w_down)
    y_sample = rmsnorm(hs, g_final)
    return (y_prompt, y_sample, p_attn_k, p_attn_v, p_conv, p_C, p_n, p_m, p_mem_k, p_mem_v,
            s_attn_k, s_attn_v, s_conv, s_C, s_n, s_m)
```

```python
import numpy as np
from contextlib import ExitStack
import concourse.bass as bass
import concourse.mybir as mybir
from concourse.bass_utils import run_bass_kernel_spmd

F32 = mybir.dt.float32
BF16 = mybir.dt.bfloat16
AF = mybir.ActivationFunctionType
ALU = mybir.AluOpType
AX = mybir.AxisListType

import os as _osx
SAME_ENGINE_SYNC = _osx.environ.get('KSES', 'raw')
NCORES = 8
WIN = dict(qm=0, km=512, vm=1024, om=1536, ig=2048, fg=2052, qa=2056, ka=2568, va=3080)
D_FF = 2816
NSEL = 896
G = 256
NG = 2048 // G
GT = G // 128


class Dep:
    __slots__ = ("name", "writer", "readers", "sem")

    def __init__(self, name, sem=None):
        self.name = name
        self.writer = None
        self.readers = []
        self.sem = sem


class DmaSem:
    __slots__ = ("name", "count", "handle")

    def __init__(self, name):
        self.name = name
        self.count = 0
        self.handle = None


class Op:
    __slots__ = ("eng", "fn", "deps", "needs_signal", "token", "dsem", "waits")


class Prog:
    ENGS = ("pe", "act", "dve", "pool", "sp")

    def __init__(self, nc):
        self.nc = nc
        self.ops = {e: [] for e in self.ENGS}
        self.dsems = []
        self.seq = []

    def dma_sem(self, name):
        s = DmaSem(name)
        self.dsems.append(s)
        return s

    def dep(self, name, dma=False):
        return Dep(name, self.dma_sem(name) if dma else None)

    def op(self, eng, fn, reads=(), writes=(), dsem=None):
        o = Op()
        o.eng = eng
        o.fn = fn
        o.dsem = dsem
        o.needs_signal = dsem is not None
        o.token = None
        deps = {}
        raw = set()
        for d in reads:
            if d.writer is not None:
                deps[id(d.writer)] = d.writer
                raw.add(id(d.writer))
        for d in writes:
            if d.writer is not None:
                deps[id(d.writer)] = d.writer
            for r in d.readers:
                deps[id(r)] = r
        dl = []
        for x in deps.values():
            if x is o:
                continue
            if x.eng == eng and x.dsem is None:
                if eng == "pe" or not SAME_ENGINE_SYNC:
                    continue
                if SAME_ENGINE_SYNC == "raw" and id(x) not in raw:
                    continue
            dl.append(x)
        o.deps = dl
        for x in dl:
            x.needs_signal = True
        for d in reads:
            d.readers.append(o)
        for d in writes:
            d.writer = o
            d.readers = []
        self.ops[eng].append(o)
        self.seq.append(o)
        return o

    def emit(self):
        for o in self.seq:
            if o.dsem is not None:
                o.dsem.count += 16
                o.token = (o.dsem, o.dsem.count)
        for e in self.ENGS:
            cnt = 0
            for o in self.ops[e]:
                if o.dsem is None and o.needs_signal:
                    cnt += 1
                    o.token = (e, cnt)
        for e in self.ENGS:
            seen = {}
            for o in self.ops[e]:
                w = {}
                for x in o.deps:
                    k, v = x.token
                    kk = id(k) if isinstance(k, DmaSem) else k
                    if seen.get(kk, 0) >= v:
                        continue
                    if kk not in w or w[kk][1] < v:
                        w[kk] = (k, v)
                for kk, (k, v) in w.items():
                    seen[kk] = v
                o.waits = list(w.values())

    def run(self, E):
        nc = self.nc
        esem = {e: E(nc.semaphore("s_" + e)) for e in self.ENGS}
        for i, s in enumerate(self.dsems):
            if s.count > 0:
                s.handle = E(nc.semaphore("d%d_%s" % (i, s.name)))
        block = E(nc.Block())

        def replay(eng, key):
            for o in self.ops[key]:
                for (k, v) in o.waits:
                    eng.wait_ge(k.handle if isinstance(k, DmaSem) else esem[k], v)
                ins = o.fn(eng)
                if o.dsem is not None:
                    ins.then_inc(o.dsem.handle, 16)
                elif o.needs_signal:
                    ins.then_inc(esem[key], 1)
            if key == "sp":
                for s in self.dsems:
                    if s.count > 0:
                        eng.wait_ge(s.handle, s.count)

        @block.tensor
        def _(eng):
            replay(eng, "pe")

        @block.scalar
        def _(eng):
            replay(eng, "act")

        @block.vector
        def _(eng):
            replay(eng, "dve")

        @block.gpsimd
        def _(eng):
            replay(eng, "pool")

        @block.sync
        def _(eng):
            replay(eng, "sp")


def _cfun(d):
    d = np.asarray(d)
    c = ((d >= 0) & (d <= 128)).astype(np.float32)
    c += ((d >= 0) & (d % 4 == 0) & (d <= 512)).astype(np.float32)
    c += ((d >= 0) & (d % 16 == 0) & (d <= 2048)).astype(np.float32)
    return c


def _sel_positions():
    p = np.arange(2048)
    keep = (p >= 1536) | (p % 16 < 4)
    return p[keep]


def host_consts():
    c = {}
    ik = np.arange(128)[:, None]
    u = np.arange(2944)[None, :] - 384
    c["c_cmask"] = _cfun(u - ik).astype(np.float32)
    s = np.arange(128)
    c["c_tri"] = (s[:, None] <= s[None, :]).astype(np.float32)
    sel = np.zeros((4, 4, 128), np.float32)
    for h in range(4):
        sel[h, h, :] = 1.0
    c["c_selh"] = sel.reshape(4, 512)
    c["c_id4"] = np.eye(4, dtype=np.float32)
    pos = _sel_positions()
    assert len(pos) == NSEL
    sm = np.zeros((128, 8, 32), np.float32)
    for kt in range(8):
        for r in range(128):
            if kt < 7:
                p = pos[kt * 128 + r]
            elif r < 4:
                p = 2048 + r
            else:
                continue
            for t in range(4):
                sm[r, kt, t::4] = _cfun(2048 + t - p)
    c["c_smask"] = sm
    i = np.arange(64)
    c["c_tri64"] = ((i[:, None] // 4 == i[None, :] // 4) & (i[:, None] <= i[None, :])).astype(np.float32)
    qm = np.zeros((128, 16, 64), np.float32)
    for sq in range(16):
        qm[:, sq, 4 * sq:4 * sq + 4] = 1.0
    c["c_seqmask"] = qm
    p32 = np.arange(32)[:, None]
    c["c_bd"] = (p32 // 4 == (np.arange(512)[None, :] // 64)).astype(np.float32)
    c["c_rowm"] = (np.arange(64)[:, None] // 4 == np.arange(16)[None, :]).astype(np.float32)
    return c


def build():
    nc = bass.Bass("TRN2", target_bir_lowering=False)
    P = Prog(nc)
    es = ExitStack()
    E = es.enter_context

    def din(name, shape):
        return nc.dram_tensor(name, list(shape), F32, kind="ExternalInput").ap()

    def dout(name, shape):
        return nc.dram_tensor(name, list(shape), F32, kind="ExternalOutput").ap()

    xblk = din("xblk", [4, 1024, 2048])
    xsT = din("xsT", [1024, 64])
    memT = din("memT", [1024, 256])
    flags = din("flags", [4, 12])
    hflag = din("hflag", [128, 1])
    Wd = {}
    for nm, sh in (("w_in", [1024, 3592]), ("w_out", [1024, 1024]), ("w_mk", [1024, 1024]), ("w_mv", [1024, 1024]),
                   ("w_mq", [1024, 1024]), ("w_mo", [1024, 1024]), ("w_gate", [1024, D_FF]), ("w_up", [1024, D_FF]),
                   ("w_down", [D_FF, 1024])):
        Wd[nm] = din(nm, sh)
    gvec = din("gvec", [5, 1024])
    conv_w = din("conv_w", [4, 1024])
    conv_b = din("conv_b", [1024])
    b_gates = din("b_gates", [8])
    g_mh = din("g_mh", [512])
    cin = {k: din(k, v.shape) for k, v in host_consts().items()}
    s_kT = din("s_kT", [16, 512, NSEL])
    s_v = din("s_v", [16, NSEL, 512])
    s_mkT = din("s_mkT", [16, 1024, 256])
    s_mv = din("s_mv", [16, 256, 1024])
    s_convT = din("s_convT", [1024, 16, 3])
    s_C0 = din("s_C0", [16, 4, 128, 128])
    s_n0T = din("s_n0T", [128, 16, 4])
    s_m0T = din("s_m0T", [4, 16])

    o_yT = dout("o_yT", [1024, 2048])
    o_ysT = dout("o_ysT", [1024, 64])
    o_kaT = dout("o_kaT", [512, 2048])
    o_va = dout("o_va", [2048, 512])
    o_convT = dout("o_convT", [1024, 3])
    o_C = dout("o_C", [4, 128, 129])
    o_m = dout("o_m", [4, 1])
    o_mkT = dout("o_mkT", [1024, 256])
    o_mv = dout("o_mv", [256, 1024])
    o_skaT = dout("o_skaT", [512, 64])
    o_sva = dout("o_sva", [64, 512])
    o_sconvT = dout("o_sconvT", [1024, 16, 3])
    o_sC = dout("o_sC", [16, 4, 128, 129])
    o_sm = dout("o_sm", [4, 16])

    Wb = {}
    dWb = {}
    for nm, ap in Wd.items():
        t = nc.dram_tensor(nm + "_bf", list(ap.shape), BF16)
        Wb[nm] = t.ap()
        dWb[nm] = P.dep("wb_" + nm, dma=True)

    class T:
        def __init__(self, name, shape, dt=F32, psum=False, dma=False):
            self.t = E(nc.psum_tensor(name, list(shape), dt) if psum else nc.sbuf_tensor(name, list(shape), dt))
            self.d = P.dep(name, dma=dma)

    class Pool:
        def __init__(self, name, n, shape, dt=F32, psum=False, dma=False):
            self.tiles = [T("%s%d" % (name, i), shape, dt, psum, dma) for i in range(n)]
            self.i = 0

        def next(self):
            t = self.tiles[self.i % len(self.tiles)]
            self.i += 1
            return t

    def I(eng, meth, *a, r=(), w=(), **kw):
        return P.op(eng, lambda e: getattr(e, meth)(*a, **kw), r, w)

    def DMA(q, out, in_, r=(), w=(), sem=None, **kw):
        return P.op(q, lambda e: e.dma_start(out=out, in_=in_, **kw), r, w, dsem=sem)

    gen = Pool("psg", 4, [128, 512], F32, psum=True)
    psS = Pool("pss", 2, [128, 512], F32, psum=True)
    psO = Pool("pso", 2, [128, 512], F32, psum=True)

    ones_s = T("ones_s", [128, 128], BF16)
    ident = T("ident", [128, 128], BF16)
    onesf = T("onesf", [128, 128], F32)
    eps_t = T("eps_t", [128, 1], F32)
    gv = T("gv", [128, 5, 8], F32, dma=True)
    cw = T("cw", [128, 8, 4], F32, dma=True)
    cb = T("cb", [128, 8], F32, dma=True)
    gmh = T("gmh", [128, 512], F32, dma=True)
    bg = T("bg", [4, 2], F32, dma=True)
    nbf = T("nbf", [4, 1], F32)
    fl4 = T("fl4", [4, 12], F32, dma=True)
    hfl = T("hfl", [128, 1], F32, dma=True)
    tri = T("tri", [128, 128], BF16, dma=True)
    selh = T("selh", [4, 512], F32, dma=True)
    id4 = T("id4", [4, 4], F32, dma=True)
    cmask = T("cmask", [128, 2944], BF16, dma=True)
    smask = T("smask", [128, 8, 32], F32, dma=True)
    tri64 = T("tri64", [64, 64], F32, dma=True)
    seqmask = T("seqmask", [128, 16, 64], BF16, dma=True)

    I("pool", "memset", ones_s.t[:], 1.0 / 1024.0, w=[ones_s.d])
    I("pool", "memset", onesf.t[:], 1.0, w=[onesf.d])
    I("pool", "memset", eps_t.t[:], 1e-6, w=[eps_t.d])
    I("pool", "memset", ident.t[:], 1.0, w=[ident.d])
    I("pool", "affine_select", ident.t[:], ident.t[:], [[-1, 128]], ALU.is_equal, 0.0, base=0, channel_multiplier=1,
      r=[ident.d], w=[ident.d])
    DMA("sp", gv.t[:], gvec.rearrange("g (k p) -> p g k", p=128), w=[gv.d], sem=gv.d.sem, allow_slow_non_contiguous=True)
    for j_ in range(4):
        DMA("sp", cw.t[:, :, j_], conv_w[j_].rearrange("(k p) -> p k", p=128), w=[cw.d], sem=cw.d.sem,
            allow_slow_non_contiguous=True)
    DMA("sp", cb.t[:], conv_b.rearrange("(k p) -> p k", p=128), w=[cb.d], sem=cb.d.sem, allow_slow_non_contiguous=True)
    DMA("sp", gmh.t[:], g_mh.partition_broadcast(128), w=[gmh.d], sem=gmh.d.sem)
    DMA("sp", bg.t[:], b_gates.rearrange("(two h) -> h two", two=2), w=[bg.d], sem=bg.d.sem, allow_slow_non_contiguous=True)
    DMA("sp", fl4.t[:], flags, w=[fl4.d], sem=fl4.d.sem)
    DMA("sp", hfl.t[:], hflag, w=[hfl.d], sem=hfl.d.sem)
    DMA("sp", selh.t[:], cin["c_selh"], w=[selh.d], sem=selh.d.sem)
    DMA("sp", id4.t[:], cin["c_id4"], w=[id4.d], sem=id4.d.sem)
    DMA("sp", smask.t[:], cin["c_smask"], w=[smask.d], sem=smask.d.sem)
    DMA("sp", tri64.t[:], cin["c_tri64"], w=[tri64.d], sem=tri64.d.sem)
    DMA("pool", tri.t[:], cin["c_tri"], w=[tri.d], sem=tri.d.sem)
    DMA("pool", cmask.t[:], cin["c_cmask"], w=[cmask.d], sem=cmask.d.sem)
    DMA("pool", seqmask.t[:], cin["c_seqmask"], w=[seqmask.d], sem=seqmask.d.sem)
    I("dve", "tensor_scalar", nbf.t[:], bg.t[:, 1:2], -1.0, None, ALU.mult, r=[bg.d], w=[nbf.d])


    wpool = Pool("wt", 3, [128, 8, 512], BF16, dma=True)
    wdpool = Pool("wdn", 3, [128, 11, 128], BF16, dma=True)
    xpool = Pool("xg", 1, [128, 8, G], F32, dma=True)
    sqpool = Pool("sq", 2, [128, G], BF16)
    rspool = Pool("rs", 2, [128, G], F32)
    hpool = Pool("hT", 2, [128, 8, G], BF16)

    def load_w(nm, c0, ncols):
        wt = wpool.next()
        DMA("sp", wt.t[:, :, 0:ncols], Wb[nm][:, c0:c0 + ncols].rearrange("(k p) c -> p k c", p=128),
            r=[dWb[nm]], w=[wt.d], sem=wt.d.sem)
        return wt

    import os as _os2
    def precast(nm):
        src, dst = Wd[nm], Wb[nm]
        rows, cols = src.shape
        for r0 in range(0, rows, 1024):
            kk = min(8, (rows - r0) // 128)
            for c0 in range(0, cols, 512):
                cw_ = min(512, cols - c0)
                st = wpool.next()
                DMA("pool", st.t[:, 0:kk, 0:cw_], src[r0:r0 + kk * 128, c0:c0 + cw_].rearrange("(k p) c -> p k c", p=128),
                    w=[st.d], sem=st.d.sem)
                DMA("pool", dst[r0:r0 + kk * 128, c0:c0 + cw_].rearrange("(k p) c -> p k c", p=128), st.t[:, 0:kk, 0:cw_],
                    r=[st.d], w=[dWb[nm]], sem=st.d.sem)

    precast("w_mk")
    precast("w_mv")

    def norm(xt, n, gi, hT):
        ps = gen.next()
        for k in range(8):
            sq = sqpool.next()
            I("act", "activation", sq.t[:, 0:n], xt.t[:, k, 0:n], AF.Square, r=[xt.d], w=[sq.d])
            I("pe", "matmul", ps.t[:, 0:n], ones_s.t[:], sq.t[:, 0:n], start=(k == 0), stop=(k == 7),
              r=[ones_s.d, sq.d], w=[ps.d])
        rs = rspool.next()
        I("act", "activation", rs.t[:, 0:n], ps.t[:, 0:n], AF.Ln, bias=eps_t.t[:, 0:1], r=[ps.d, eps_t.d], w=[rs.d])
        I("act", "activation", rs.t[:, 0:n], rs.t[:, 0:n], AF.Exp, scale=-0.5, r=[rs.d], w=[rs.d])
        if hT is None:
            return rs
        for k in range(8):
            I("dve", "scalar_tensor_tensor", hT.t[:, k, 0:n], xt.t[:, k, 0:n], gv.t[:, gi, k:k + 1], rs.t[:, 0:n],
              ALU.mult, ALU.mult, r=[xt.d, gv.d, rs.d], w=[hT.d])

    def proj_fm(nm, c0, ncols, hT, n, evac, K=8):
        wt = load_w(nm, c0, ncols)
        ckpt("lw")
        for cc in range((ncols + 127) // 128):
            m = min(128, ncols - cc * 128)
            ps = gen.next()
            for k in range(K):
                I("pe", "matmul", ps.t[0:m, 0:n], wt.t[:, k, cc * 128:cc * 128 + m], hT.t[:, k, 0:n],
                  start=(k == 0), stop=(k == K - 1), r=[wt.d, hT.d], w=[ps.d])
            ckpt("mm")
            evac(cc, ps, m)
            ckpt("ev")

    def proj_tm(nm, c0, ncols, hT, n, evac):
        wt = load_w(nm, c0, ncols)
        for tt in range((n + 127) // 128):
            m = min(128, n - tt * 128)
            ps = gen.next()
            for k in range(8):
                I("pe", "matmul", ps.t[0:m, 0:ncols], hT.t[:, k, tt * 128:tt * 128 + m], wt.t[:, k, 0:ncols],
                  start=(k == 0), stop=(k == 7), r=[wt.d, hT.d], w=[ps.d])
            evac(tt, ps, m)

    kstg = Pool("kstg", 2, [128, 512], F32, dma=True)

    class _Stop(Exception):
        pass

    def ckpt(name):
        if _os2.environ.get("KSTOP") == name:
            raise _Stop()

    try:
        mkT = T("mkT", [128, 8, 256], BF16)
        mv = T("mv", [128, 2, 1024], BF16)
        xm = xpool.next()
        DMA("sp", xm.t[:, :, 0:256], memT.rearrange("(k p) n -> p k n", p=128), w=[xm.d], sem=xm.d.sem)
        hm_ = hpool.next()
        norm(xm, 256, 1, hm_)
        for half in range(2):
            def ev_mk(cc, ps, m, half=half):
                ch = half * 4 + cc
                st = kstg.next()
                I("dve", "tensor_copy", st.t[:, 0:256], ps.t[:, 0:256], r=[ps.d], w=[st.d])
                I("act", "mul", mkT.t[:, ch, :], st.t[:, 0:256], 1.0, r=[st.d], w=[mkT.d])
                DMA("sp", o_mkT[ch * 128:(ch + 1) * 128, :], st.t[:, 0:256], r=[st.d], sem=st.d.sem)
            proj_fm("w_mk", half * 512, 512, hm_, 256, ev_mk)
            ckpt("mk%d" % half)
        for half in range(2):
            def ev_mv(tt, ps, m, half=half):
                st = kstg.next()
                I("dve", "tensor_copy", st.t[:, 0:512], ps.t[:, 0:512], r=[ps.d], w=[st.d])
                I("act", "mul", mv.t[:, tt, half * 512:(half + 1) * 512], st.t[:, 0:512], 1.0, r=[st.d], w=[mv.d])
                DMA("sp", o_mv[tt * 128:(tt + 1) * 128, half * 512:(half + 1) * 512], st.t[:, 0:512],
                    r=[st.d], sem=st.d.sem)
            proj_tm("w_mv", half * 512, 512, hm_, 256, ev_mv)
            ckpt("mv%d" % half)

        ckpt("mem")
        kaT = T("kaT", [128, 4, 4096], BF16)
        vaa = T("vaa", [128, 32, 8, 65], BF16)
        I("pool", "memset", vaa.t[:], 1.0, w=[vaa.d])
        I("dve", "tensor_scalar", vaa.t[:, 0:16, :, 64:65], vaa.t[:, 0:16, :, 64:65], hfl.t[:, 0:1], None, ALU.mult,
          r=[vaa.d, hfl.d], w=[vaa.d])
        dka = [P.dep("ka%d" % i) for i in range(2 * NG)]
        dva = [P.dep("va%d" % i) for i in range(2 * NG)]
        for d_ in dva:
            d_.writer = vaa.d.writer
        u = T("u", [128, 8, G + 3], F32, dma=True)
        I("pool", "memset", u.t[:], 0.0, w=[u.d])
        Cst = T("Cst", [128, 4, 129], F32, dma=True)
        Cb = T("Cb", [128, 4, 129], BF16)
        dC = [P.dep("dC%d" % h_) for h_ in range(4)]
        dCb = [P.dep("dCb%d" % h_) for h_ in range(4)]
        I("pool", "memset", Cst.t[:], 0.0, w=[Cst.d] + dC)
        Bc = T("Bc", [4, 1], F32)
        Mc = T("Mc", [4, 1], F32)
        I("pool", "memset", Bc.t[:], 0.0, w=[Bc.d])
        I("pool", "memset", Mc.t[:], 0.0, w=[Mc.d])
        one4 = T("one4", [4, 1], F32)
        I("pool", "memset", one4.t[:], 1.0, w=[one4.d])

        qmT = T("qmT", [128, 4, G], BF16)
        kmT = T("kmT", [128, 4, G], BF16)
        ktT = T("ktT", [128, 4, G], BF16)
        vma = T("vma", [128, GT, 4, 129], BF16)
        I("pool", "memset", vma.t[:], 1.0, w=[vma.d])
        omS = T("omS", [128, GT, 512], BF16)
        cv = Pool("cv", 2, [128, G], F32)
        gz = T("gz", [4, 2, G], F32)
        g_e = T("g_e", [4, G], F32)
        g_B = T("g_B", [4, G], F32)
        g_G = T("g_G", [4, G], F32)
        g_ws = T("g_ws", [4, G], F32)
        g_fl = T("g_fl", [4, G], F32)
        g_s = T("g_s", [4, 24], F32)
        decB = T("decB", [128, 16], F32)
        floorT = T("floorT", [128, 16], F32)
        ktp = Pool("kt", 4, [128, 128], BF16)
        sdp = Pool("sd", 2, [128, 128], BF16)
        nhp = Pool("nh", 4, [128, 129], F32)
        hrp = Pool("hr", 2, [128, 128], F32)
        sqp = Pool("sq2", 2, [128, 128], F32)
        stp = Pool("stt", 2, [128, 12], F32)
        hmp = Pool("hmb", 4, [128, 128], BF16)
        hmT = T("hmT", [128, 4, G], BF16)

        def gates(n, blk, full):
            nch = n // 128
            zi = gz.t[0:4, 0, 0:n]
            zf = gz.t[0:4, 1, 0:n]
            I("act", "activation", g_e.t[:, 0:n], zf, AF.Exp, bias=nbf.t[:, 0:1], scale=-1.0, r=[gz.d, nbf.d], w=[g_e.d])
            I("act", "activation", g_e.t[:, 0:n], g_e.t[:, 0:n], AF.Ln, bias=one4.t[:, 0:1], r=[g_e.d, one4.d], w=[g_e.d])
            I("dve", "tensor_scalar", g_e.t[:, 0:n], g_e.t[:, 0:n], fl4.t[:, 8 + blk:9 + blk], None, ALU.mult,
              r=[g_e.d, fl4.d], w=[g_e.d])
            I("dve", "tensor_tensor_scan", g_B.t[:, 0:n], g_e.t[:, 0:n], g_e.t[:, 0:n], Bc.t[:, 0:1], ALU.add, ALU.bypass,
              r=[g_e.d, Bc.d], w=[g_B.d])
            I("dve", "tensor_scalar", g_G.t[:, 0:n], zi, bg.t[:, 0:1], None, ALU.add, r=[gz.d, bg.d], w=[g_G.d])
            I("dve", "tensor_scalar", g_G.t[:, 0:n], g_G.t[:, 0:n], fl4.t[:, blk:blk + 1], fl4.t[:, 4 + blk:5 + blk],
              ALU.mult, ALU.add, r=[g_G.d, fl4.d], w=[g_G.d])
            I("dve", "tensor_sub", g_G.t[:, 0:n], g_G.t[:, 0:n], g_B.t[:, 0:n], r=[g_G.d, g_B.d], w=[g_G.d])
            I("dve", "tensor_reduce", g_s.t[:, 0:nch], g_G.t[:, 0:n].rearrange("p (c t) -> p c t", t=128), AX.X, ALU.max,
              r=[g_G.d], w=[g_s.d])
            I("dve", "tensor_tensor_scan", g_s.t[:, 4:4 + nch], g_s.t[:, 0:nch], g_s.t[:, 0:nch], Mc.t[:, 0:1], ALU.max,
              ALU.bypass, r=[g_s.d, Mc.d], w=[g_s.d])
            I("dve", "tensor_copy", g_s.t[:, 8:9], Mc.t[:, 0:1], r=[Mc.d, g_s.d], w=[g_s.d])
            if nch > 1:
                I("dve", "tensor_copy", g_s.t[:, 9:8 + nch], g_s.t[:, 4:3 + nch], r=[g_s.d], w=[g_s.d])
            I("dve", "tensor_scalar", g_s.t[:, 12:12 + nch], g_s.t[:, 4:4 + nch], -1.0, None, ALU.mult, r=[g_s.d], w=[g_s.d])
            I("dve", "tensor_sub", g_s.t[:, 16:16 + nch], g_s.t[:, 8:8 + nch], g_s.t[:, 4:4 + nch], r=[g_s.d], w=[g_s.d])
            I("act", "activation", g_s.t[:, 20:20 + nch], g_s.t[:, 16:16 + nch], AF.Exp, r=[g_s.d], w=[g_s.d])
            for c in range(nch):
                I("act", "activation", g_ws.t[:, c * 128:(c + 1) * 128], g_G.t[:, c * 128:(c + 1) * 128], AF.Exp,
                  bias=g_s.t[:, 12 + c:13 + c], r=[g_G.d, g_s.d], w=[g_ws.d])
                if full:
                    I("act", "activation", g_fl.t[:, c * 128:(c + 1) * 128], g_B.t[:, c * 128:(c + 1) * 128], AF.Exp,
                      bias=g_s.t[:, 12 + c:13 + c], scale=-1.0, r=[g_B.d, g_s.d], w=[g_fl.d])
            I("dve", "tensor_copy", Bc.t[:, 0:1], g_B.t[:, n - 1:n], r=[g_B.d], w=[Bc.d])
            I("dve", "tensor_copy", Mc.t[:, 0:1], g_s.t[:, 3 + nch:4 + nch], r=[g_s.d], w=[Mc.d])
            for h in range(4):
                ps = gen.next()
                I("pe", "matmul", ps.t[:, 0:n], selh.t[0:4, h * 128:(h + 1) * 128], g_ws.t[0:4, 0:n], start=True, stop=True,
                  r=[selh.d, g_ws.d], w=[ps.d])
                I("dve", "scalar_tensor_tensor", ktT.t[:, h, 0:n], ps.t[:, 0:n], 128.0 ** -0.5, kmT.t[:, h, 0:n],
                  ALU.mult, ALU.mult, r=[ps.d, kmT.d], w=[ktT.d])
            ps = gen.next()
            for h in range(4):
                I("pe", "matmul", ps.t[:, 4 * h:4 * h + nch], selh.t[0:4, h * 128:(h + 1) * 128], g_s.t[0:4, 20:20 + nch],
                  start=True, stop=True, r=[selh.d, g_s.d], w=[ps.d])
            I("dve", "tensor_copy", decB.t[:, :], ps.t[:, 0:16], r=[ps.d], w=[decB.d])
            if full:
                ps = gen.next()
                for c in range(nch):
                    I("pe", "matmul", ps.t[:, 4 * c:4 * c + 4], g_fl.t[0:4, c * 128:(c + 1) * 128], id4.t[0:4, 0:4],
                      start=True, stop=True, r=[g_fl.d, id4.d], w=[ps.d])
                I("dve", "tensor_copy", floorT.t[:, 0:4 * nch], ps.t[:, 0:4 * nch], r=[ps.d], w=[floorT.d])

        def ln_post(L, nh, fl_ap, fl_dep, gm_ap, om_ap, om_dep, out_bf):
            st = stp.next()
            s = st.t
            I("dve", "scalar_tensor_tensor", s[0:L, 0:1], nh.t[0:L, 128:129], -1.0, nh.t[0:L, 128:129], ALU.mult, ALU.max,
              r=[nh.d], w=[st.d])
            I("dve", "tensor_tensor", s[0:L, 0:1], s[0:L, 0:1], fl_ap, ALU.max, r=[st.d, fl_dep], w=[st.d])
            I("dve", "reciprocal", s[0:L, 1:2], s[0:L, 0:1], r=[st.d], w=[st.d])
            hr = hrp.next()
            I("dve", "tensor_scalar", hr.t[0:L, :], nh.t[0:L, 0:128], s[0:L, 1:2], None, ALU.mult, r=[nh.d, st.d], w=[hr.d])
            sq = sqp.next()
            I("act", "activation", sq.t[0:L, :], hr.t[0:L, :], AF.Square, r=[hr.d], w=[sq.d])
            I("dve", "tensor_reduce", s[0:L, 2:3], hr.t[0:L, :], AX.X, ALU.add, r=[hr.d, st.d], w=[st.d])
            I("dve", "tensor_reduce", s[0:L, 3:4], sq.t[0:L, :], AX.X, ALU.add, r=[sq.d, st.d], w=[st.d])
            I("dve", "tensor_scalar", s[0:L, 4:5], s[0:L, 2:3], 1.0 / 128.0, None, ALU.mult, r=[st.d], w=[st.d])
            I("dve", "tensor_mul", s[0:L, 5:6], s[0:L, 4:5], s[0:L, 4:5], r=[st.d], w=[st.d])
            I("dve", "scalar_tensor_tensor", s[0:L, 6:7], s[0:L, 3:4], 1.0 / 128.0, s[0:L, 5:6], ALU.mult, ALU.subtract,
              r=[st.d], w=[st.d])
            I("dve", "tensor_scalar_max", s[0:L, 6:7], s[0:L, 6:7], 0.0, r=[st.d], w=[st.d])
            I("act", "activation", s[0:L, 7:8], s[0:L, 6:7], AF.Ln, bias=eps_t.t[0:L, 0:1], r=[st.d, eps_t.d], w=[st.d])
            I("act", "activation", s[0:L, 7:8], s[0:L, 7:8], AF.Exp, scale=-0.5, r=[st.d], w=[st.d])
            I("dve", "scalar_tensor_tensor", s[0:L, 8:9], s[0:L, 4:5], -1.0, s[0:L, 7:8], ALU.mult, ALU.mult,
              r=[st.d], w=[st.d])
            I("dve", "tensor_scalar", hr.t[0:L, :], hr.t[0:L, :], s[0:L, 7:8], s[0:L, 8:9], ALU.mult, ALU.add,
              r=[hr.d, st.d], w=[hr.d])
            I("dve", "tensor_mul", hr.t[0:L, :], hr.t[0:L, :], gm_ap, r=[hr.d, gmh.d], w=[hr.d])
            I("dve", "tensor_mul", out_bf.t[0:L, :], hr.t[0:L, :], om_ap, r=[hr.d, om_dep], w=[out_bf.d])

        def mlstm_units(n, full):
            units = []
            for c in range(n // 128):
                st_ = {}

                def partA(c=c, st_=st_):
                    cs = slice(c * 128, (c + 1) * 128)
                    kts_, nhs = [], []
                    for h in range(4):
                        I("dve", "tensor_scalar", Cst.t[:, h, :], Cst.t[:, h, :], decB.t[:, 4 * h + c:4 * h + c + 1], None,
                          ALU.mult, r=[dC[h], decB.d], w=[dC[h]])
                        pT = gen.next()
                        I("pe", "matmul", pT.t[:, 0:128], ktT.t[:, h, cs], ident.t[:], start=True, stop=True,
                          r=[ktT.d, ident.d], w=[pT.d])
                        kt = ktp.next()
                        I("act", "mul", kt.t[:], pT.t[:, 0:128], 1.0, r=[pT.d], w=[kt.d])
                        kts_.append(kt)
                        if full:
                            I("act", "mul", Cb.t[:, h, :], Cst.t[:, h, :], 1.0, r=[dC[h]], w=[dCb[h]])
                            pS = gen.next()
                            I("pe", "matmul", pS.t[:, 0:128], ktT.t[:, h, cs], qmT.t[:, h, cs], start=True, stop=True,
                              r=[ktT.d, qmT.d], w=[pS.d])
                            sd = sdp.next()
                            I("dve", "tensor_tensor", sd.t[:], pS.t[:, 0:128], tri.t[:], ALU.mult, r=[pS.d, tri.d], w=[sd.d])
                            pN = gen.next()
                            I("pe", "matmul", pN.t[:, 0:129], sd.t[:], vma.t[:, c, h, :], start=True, stop=False,
                              r=[sd.d, vma.d], w=[pN.d])
                            I("pe", "matmul", pN.t[:, 0:129], qmT.t[:, h, cs], Cb.t[:, h, :], start=False, stop=True,
                              r=[qmT.d, dCb[h]], w=[pN.d])
                            nh = nhp.next()
                            I("act", "mul", nh.t[:], pN.t[:, 0:129], 1.0, r=[pN.d], w=[nh.d])
                            nhs.append(nh)
                    for h in range(4):
                        pU = gen.next()
                        I("pe", "matmul", pU.t[:, 0:129], kts_[h].t[:], vma.t[:, c, h, :], start=True, stop=True,
                          r=[kts_[h].d, vma.d], w=[pU.d])
                        I("dve", "tensor_add", Cst.t[:, h, :], Cst.t[:, h, :], pU.t[:, 0:129], r=[dC[h], pU.d], w=[dC[h]])
                    st_["nhs"] = nhs

                def partB(c=c, st_=st_):
                    hbs = []
                    for h in range(4):
                        hb = hmp.next()
                        ln_post(128, st_["nhs"][h], floorT.t[:, 4 * c + h:4 * c + h + 1], floorT.d,
                                gmh.t[:, h * 128:(h + 1) * 128], omS.t[:, c, h * 128:(h + 1) * 128], omS.d, hb)
                        hbs.append(hb)
                    st_["hbs"] = hbs

                def partC(c=c, st_=st_):
                    cs = slice(c * 128, (c + 1) * 128)
                    for h in range(4):
                        pT2 = gen.next()
                        I("pe", "matmul", pT2.t[:, 0:128], st_["hbs"][h].t[:], ident.t[:], start=True, stop=True,
                          r=[st_["hbs"][h].d, ident.d], w=[pT2.d])
                        I("act", "mul", hmT.t[:, h, cs], pT2.t[:, 0:128], 1.0, r=[pT2.d], w=[hmT.d])
                units.append(partA)
                if full:
                    units.append(partB)
                    units.append(partC)
            return units

        def mlstm_group(n, full):
            for f_ in mlstm_units(n, full):
                f_()

        def conv_chunk(ch, n, out_bf_ap, out_dep):
            c_ = cv.next()
            I("dve", "tensor_scalar", c_.t[:, 0:n], u.t[:, ch, 0:n], cw.t[:, ch, 0:1], cb.t[:, ch:ch + 1], ALU.mult, ALU.add,
              r=[u.d, cw.d, cb.d], w=[c_.d])
            for j in range(1, 4):
                I("dve", "scalar_tensor_tensor", c_.t[:, 0:n], u.t[:, ch, j:j + n], cw.t[:, ch, j:j + 1], c_.t[:, 0:n],
                  ALU.mult, ALU.add, r=[u.d, cw.d, c_.d], w=[c_.d])
            I("act", "activation", out_bf_ap, c_.t[:, 0:n], AF.Silu, r=[c_.d], w=[out_dep])

        qaT = T("qaT", [128, 4, G], BF16)
        ptp = Pool("pT", 4, [128, G], BF16)
        pmp = Pool("pM", 3, [128, G], BF16)
        osb = Pool("osb", 2, [65, G], F32)
        rdn = Pool("rdn", 2, [128, G], F32)
        haT = T("haT", [64, 8, G], BF16)
        NKT = 16 + GT

        def attention_prompt(g, hook=None):
            kts = list(range(GT * g, GT * g + NKT))
            jobs = [(h, i_, kt) for h in range(8) for i_, kt in enumerate(kts)]
            pss = {}

            def emitS(j):
                h, i_, kt = jobs[j]
                hc, pb = h // 2, 64 * (h % 2)
                ps = psS.next()
                I("pe", "matmul", ps.t[:, 0:G], kaT.t[pb:pb + 64, hc, kt * 128:(kt + 1) * 128], qaT.t[pb:pb + 64, hc, :],
                  start=True, stop=True, r=[dka[kt // GT], qaT.d], w=[ps.d])
                pss[j] = ps
            DEPTH = 2
            for j in range(min(DEPTH, len(jobs))):
                emitS(j)
            po = None
            for j, (h, i_, kt) in enumerate(jobs):
                if i_ == 0:
                    po = psO.next()
                ps = pss.pop(j)
                pt = ptp.next()
                I("act", "activation", pt.t[:], ps.t[:, 0:G], AF.Exp, scale=0.125, r=[ps.d], w=[pt.d])
                pm = pmp.next()
                u0 = (2048 + G * g) - 128 * kt + 384
                I("dve", "tensor_tensor", pm.t[:], pt.t[:], cmask.t[:, u0:u0 + G], ALU.mult, r=[pt.d, cmask.d], w=[pm.d])
                I("pe", "matmul", po.t[0:65, 0:G], vaa.t[:, kt, h, :], pm.t[:], start=(i_ == 0), stop=(i_ == len(kts) - 1),
                  r=[dva[kt // GT], pm.d], w=[po.d])
                if j + DEPTH < len(jobs):
                    emitS(j + DEPTH)
                if i_ == len(kts) - 1:
                    ob = osb.next()
                    I("act", "mul", ob.t[:], po.t[0:65, 0:G], 1.0, r=[po.d], w=[ob.d])
                    pd = gen.next()
                    I("pe", "matmul", pd.t[0:64, 0:G], onesf.t[64:65, 0:64], ob.t[64:65, :], start=True, stop=True,
                      r=[onesf.d, ob.d], w=[pd.d])
                    rd = rdn.next()
                    I("dve", "reciprocal", rd.t[0:64, :], pd.t[0:64, 0:G], r=[pd.d], w=[rd.d])
                    I("dve", "tensor_mul", haT.t[:, h, :], ob.t[0:64, :], rd.t[0:64, :], r=[ob.d, rd.d], w=[haT.d])
                    if hook is not None:
                        hook(h)

        aT = T("aT", [128, 11, G], BF16)
        sgp = Pool("sg", 2, [128, G], F32)
        ysb = Pool("ysb", 2, [128, G], F32, dma=True)
        ones1 = T("ones1", [128, 128], BF16)
        I("pool", "memset", ones1.t[:], 1.0, w=[ones1.d])

        def resid_proj(nm, xt, n, srcs, att=False):
            for half in range(2):
                wt = load_w(nm, half * 512, 512)
                if att:
                    wa = wpool.next()
                    DMA("sp", wa.t[0:64, :, :], Wb["w_out"][512:1024, half * 512:(half + 1) * 512].rearrange("(h p) c -> p h c", p=64),
                        r=[dWb["w_out"]], w=[wa.d], sem=wa.d.sem)
                for cc in range(4):
                    ps = gen.next()
                    nk = len(srcs)
                    for i_, (kind, k, rhs, dep) in enumerate(srcs):
                        if kind == "w":
                            lhs, ld = wt.t[:, k, cc * 128:(cc + 1) * 128], wt.d
                        else:
                            lhs, ld = wa.t[0:64, k, cc * 128:(cc + 1) * 128], wa.d
                        I("pe", "matmul", ps.t[:, 0:n], lhs, rhs, start=(i_ == 0), stop=(i_ == nk - 1), r=[ld, dep], w=[ps.d])
                    ch = half * 4 + cc
                    I("dve", "tensor_add", xt.t[:, ch, 0:n], xt.t[:, ch, 0:n], ps.t[:, 0:n], r=[xt.d, ps.d], w=[xt.d])

        def xattn(xt, n, mk_ap_fn, mv_ap_fn, mdeps, per_seq, pre=None):
            h1 = hpool.next()
            norm(xt, n, 2, h1)
            qx = hpool.next()
            for half in range(2):
                def ev_q(cc, ps, m, half=half):
                    I("act", "mul", qx.t[:, half * 4 + cc, 0:n], ps.t[:, 0:n], 1.0, r=[ps.d], w=[qx.d])
                proj_fm("w_mq", half * 512, 512, h1, n, ev_q)
            oT = hpool.next()
            groups = [(0, n)] if not per_seq else [(4 * s_, 4) for s_ in range(16)]
            jobs = [(gi, t0, tn, h) for gi, (t0, tn) in enumerate(groups) for h in range(4)]

            def emit_scores(job):
                gi, t0, tn, h = job
                if pre is not None and h == 0:
                    pre(gi)
                pts = []
                for mb in range(2):
                    ps = psS.next()
                    for ec in range(2):
                        I("pe", "matmul", ps.t[:, 0:tn], mk_ap_fn(gi, h, ec, mb), qx.t[:, 2 * h + ec, t0:t0 + tn],
                          start=(ec == 0), stop=(ec == 1), r=mdeps(gi) + [qx.d], w=[ps.d])
                    pt = ptp.next()
                    I("act", "activation", pt.t[:, 0:tn], ps.t[:, 0:tn], AF.Exp, scale=1.0 / 16.0, r=[ps.d], w=[pt.d])
                    pts.append(pt)
                return pts
            pend = emit_scores(jobs[0])
            for ji, (gi, t0, tn, h) in enumerate(jobs):
                pts = pend
                if ji + 1 < len(jobs):
                    pend = emit_scores(jobs[ji + 1])
                pd = gen.next()
                for mb in range(2):
                    I("pe", "matmul", pd.t[:, 0:tn], ones1.t[:], pts[mb].t[:, 0:tn], start=(mb == 0), stop=(mb == 1),
                      r=[ones1.d, pts[mb].d], w=[pd.d])
                rd = rdn.next()
                I("dve", "reciprocal", rd.t[:, 0:tn], pd.t[:, 0:tn], r=[pd.d], w=[rd.d])
                for ec in range(2):
                    po = gen.next()
                    for mb in range(2):
                        I("pe", "matmul", po.t[:, 0:tn], mv_ap_fn(gi, h, ec, mb), pts[mb].t[:, 0:tn],
                          start=(mb == 0), stop=(mb == 1), r=mdeps(gi) + [pts[mb].d], w=[po.d])
                    I("dve", "tensor_mul", oT.t[:, 2 * h + ec, t0:t0 + tn], po.t[:, 0:tn], rd.t[:, 0:tn],
                      r=[po.d, rd.d], w=[oT.d])
            resid_proj("w_mo", xt, n, [("w", k, oT.t[:, k, 0:n], oT.d) for k in range(8)])

        def ffn(xt, n):
            h2 = hpool.next()
            norm(xt, n, 3, h2)
            for hf in range(2):
                base = hf * 1408
                for c0 in (0, 512, 1024):
                    ncols = min(512, 1408 - c0)
                    wg = load_w("w_gate", base + c0, ncols)
                    wu = load_w("w_up", base + c0, ncols)
                    for cc in range(ncols // 128):
                        f = c0 // 128 + cc
                        pg = gen.next()
                        pu = gen.next()
                        for k in range(8):
                            I("pe", "matmul", pg.t[:, 0:n], wg.t[:, k, cc * 128:(cc + 1) * 128], h2.t[:, k, 0:n],
                              start=(k == 0), stop=(k == 7), r=[wg.d, h2.d], w=[pg.d])
                        for k in range(8):
                            I("pe", "matmul", pu.t[:, 0:n], wu.t[:, k, cc * 128:(cc + 1) * 128], h2.t[:, k, 0:n],
                              start=(k == 0), stop=(k == 7), r=[wu.d, h2.d], w=[pu.d])
                        sg = sgp.next()
                        I("act", "activation", sg.t[:, 0:n], pg.t[:, 0:n], AF.Silu, r=[pg.d], w=[sg.d])
                        I("dve", "tensor_mul", aT.t[:, f, 0:n], sg.t[:, 0:n], pu.t[:, 0:n], r=[sg.d, pu.d], w=[aT.d])
                for ch in range(8):
                    wd = wdpool.next()
                    DMA("sp", wd.t[:], Wb["w_down"][base:base + 1408, ch * 128:(ch + 1) * 128].rearrange("(k p) c -> p k c", p=128),
                        r=[dWb["w_down"]], w=[wd.d], sem=wd.d.sem)
                    ps = gen.next()
                    for k in range(11):
                        I("pe", "matmul", ps.t[:, 0:n], wd.t[:, k, :], aT.t[:, k, 0:n],
                          start=(k == 0), stop=(k == 10), r=[wd.d, aT.d], w=[ps.d])
                    I("dve", "tensor_add", xt.t[:, ch, 0:n], xt.t[:, ch, 0:n], ps.t[:, 0:n], r=[xt.d, ps.d], w=[xt.d])

        def final_out(xt, n, out_fn):
            rs = norm(xt, n, 4, None)
            for k in range(8):
                y = ysb.next()
                I("dve", "scalar_tensor_tensor", y.t[:, 0:n], xt.t[:, k, 0:n], gv.t[:, 4, k:k + 1], rs.t[:, 0:n],
                  ALU.mult, ALU.mult, r=[xt.d, gv.d, rs.d], w=[y.d])
                DMA("sp", out_fn(k), y.t[:, 0:n], r=[y.d], sem=y.d.sem)

        KD = _os2.environ.get("KDUMP", "")

        def dump_rows(src_ap, dep, row0, nrows, g):
            y = ysb.next()
            I("dve", "tensor_copy", y.t[0:nrows, 0:G], src_ap, r=[dep], w=[y.d])
            DMA("sp", o_yT[row0:row0 + nrows, g * G:(g + 1) * G], y.t[0:nrows, 0:G], r=[y.d], sem=y.d.sem)

        precast("w_in")
        _late = ["w_out", "w_mq", "w_mo", "w_gate", "w_up", "w_down"]
        _pre_x = [None]
        for blk in range(int(_os2.environ.get('KBLK0', '0')), 4):
            local = blk == 3
            halo = blk == 2
            for g in range(NG):
                if _pre_x[0] is not None:
                    xt = _pre_x[0]
                    _pre_x[0] = None
                else:
                    xt = xpool.next()
                    DMA("sp", xt.t[:], xblk[blk, :, g * G:(g + 1) * G].rearrange("(k p) n -> p k n", p=128),
                        w=[xt.d], sem=xt.d.sem)
                hT = hpool.next()
                norm(xt, G, 0, hT)
                if not local:
                    nb_, ng_ = (blk, g + 1) if g + 1 < NG else (blk + 1, 0)
                    nx = xpool.next()
                    DMA("pool", nx.t[:], xblk[nb_, :, ng_ * G:(ng_ + 1) * G].rearrange("(k p) n -> p k n", p=128),
                        w=[nx.d], sem=nx.d.sem)
                    _pre_x[0] = nx
                ckpt('g_norm')
                need_q = local or (halo and g == NG - 1)
                if need_q:
                    def ev_u(cc, ps, m):
                        I("act", "mul", u.t[:, cc, 3:3 + G], ps.t[:, 0:G], 1.0, r=[ps.d], w=[u.d])
                    proj_fm("w_in", WIN["qm"], 512, hT, G, ev_u)

                def ev_uk(cc, ps, m):
                    I("act", "mul", u.t[:, 4 + cc, 3:3 + G], ps.t[:, 0:G], 1.0, r=[ps.d], w=[u.d])
                proj_fm("w_in", WIN["km"], 512, hT, G, ev_uk)
                for ch in range(8):
                    if ch < 4 and not local:
                        continue
                    if ch < 4:
                        conv_chunk(ch, G, qmT.t[:, ch, :], qmT.d)
                    else:
                        conv_chunk(ch, G, kmT.t[:, ch - 4, :], kmT.d)
                if local and g == NG - 1:
                    DMA("sp", o_convT.rearrange("(k p) j -> p k j", p=128), u.t[:, :, G:G + 3], r=[u.d], sem=u.d.sem,
                        allow_slow_non_contiguous=True)
                I("dve", "tensor_copy", u.t[:, :, 0:3], u.t[:, :, G:G + 3], r=[u.d], w=[u.d])
                ckpt('g_conv')

                def ev_vm(tt, ps, m):
                    I("act", "mul", vma.t[:, tt, :, 0:128], ps.t[:, 0:512].rearrange("p (h e) -> p h e", e=128), 1.0,
                      r=[ps.d], w=[vma.d])
                proj_tm("w_in", WIN["vm"], 512, hT, G, ev_vm)
                if local:
                    def ev_om(tt, ps, m):
                        I("act", "activation", omS.t[:, tt, :], ps.t[:, 0:512], AF.Sigmoid, r=[ps.d], w=[omS.d])
                    proj_tm("w_in", WIN["om"], 512, hT, G, ev_om)
                wt = load_w("w_in", WIN["ig"], 8)
                for gi_ in range(2):
                    ps = gen.next()
                    for k in range(8):
                        I("pe", "matmul", ps.t[0:4, 0:G], wt.t[:, k, 4 * gi_:4 * gi_ + 4], hT.t[:, k, :],
                          start=(k == 0), stop=(k == 7), r=[wt.d, hT.d], w=[ps.d])
                    I("act", "mul", gz.t[:, gi_, :], ps.t[0:4, 0:G], 1.0, r=[ps.d], w=[gz.d])
                gates(G, blk, local)
                ckpt('g_gates')
                if local:
                    _units = mlstm_units(G, True)
                else:
                    mlstm_group(G, False)
                ckpt('g_mlstm')
                if _late:
                    precast(_late.pop(0))
                if halo or local:
                    kg = (0 if halo else NG) + g

                    def ev_ka(cc, ps, m, kg=kg):
                        if local:
                            st = kstg.next()
                            I("dve", "tensor_copy", st.t[:, 0:G], ps.t[:, 0:G], r=[ps.d], w=[st.d])
                            I("act", "mul", kaT.t[:, cc, kg * G:(kg + 1) * G], st.t[:, 0:G], 1.0,
                              r=[st.d], w=[dka[kg]])
                            DMA("sp", o_kaT[cc * 128:(cc + 1) * 128, g * G:(g + 1) * G], st.t[:, 0:G], r=[st.d], sem=st.d.sem)
                        else:
                            I("act", "mul", kaT.t[:, cc, kg * G:(kg + 1) * G], ps.t[:, 0:G], 1.0,
                              r=[ps.d], w=[dka[kg]])
                    proj_fm("w_in", WIN["ka"], 512, hT, G, ev_ka)

                    def ev_va(tt, ps, m, kg=kg):
                        if local:
                            st = kstg.next()
                            I("dve", "tensor_copy", st.t[:], ps.t[:, 0:512], r=[ps.d], w=[st.d])
                            I("act", "mul", vaa.t[:, kg * GT + tt, :, 0:64],
                              st.t[:, 0:512].rearrange("p (h e) -> p h e", e=64), 1.0, r=[st.d], w=[dva[kg]])
                            DMA("sp", o_va[g * G + tt * 128:g * G + (tt + 1) * 128, :], st.t[:], r=[st.d], sem=st.d.sem)
                        else:
                            I("dve", "tensor_scalar", vaa.t[:, kg * GT + tt, :, 0:64],
                              ps.t[:, 0:512].rearrange("p (h e) -> p h e", e=64), hfl.t[:, 0:1], None, ALU.mult,
                              r=[ps.d, hfl.d], w=[dva[kg]])
                    proj_tm("w_in", WIN["va"], 512, hT, G, ev_va)
                    ckpt('g_kv')
                if local:
                    def ev_qa(cc, ps, m):
                        I("act", "mul", qaT.t[:, cc, :], ps.t[:, 0:G], 1.0, r=[ps.d], w=[qaT.d])
                    proj_fm("w_in", WIN["qa"], 512, hT, G, ev_qa)
                    def _hook(h_, _units=_units):
                        if _units:
                            _units.pop(0)()
                    attention_prompt(g, _hook)
                    while _units:
                        _units.pop(0)()
                    ckpt('g_attn')
                    if KD == "qk":
                        for k in range(4):
                            dump_rows(qmT.t[:, k, :], qmT.d, k * 128, 128, g)
                            dump_rows(kmT.t[:, k, :], kmT.d, 512 + k * 128, 128, g)
                        continue
                    if KD == "h":
                        for k in range(8):
                            dump_rows(hT.t[:, k, :], hT.d, k * 128, 128, g)
                        continue
                    if KD == "mix":
                        for k in range(4):
                            dump_rows(hmT.t[:, k, :], hmT.d, k * 128, 128, g)
                        for h_ in range(8):
                            dump_rows(haT.t[:, h_, :], haT.d, 512 + h_ * 64, 64, g)
                        continue
                    srcs = [("w", k, hmT.t[:, k, :], hmT.d) for k in range(4)] + \
                           [("a", h, haT.t[:, h, :], haT.d) for h in range(8)]
                    resid_proj("w_out", xt, G, srcs, att=True)
                    ckpt('g_wout')
                    if KD == "x1":
                        for k in range(8):
                            dump_rows(xt.t[:, k, :], xt.d, k * 128, 128, g)
                        continue
                    xattn(xt, G,
                          lambda gi, h, ec, mb: mkT.t[:, 2 * h + ec, mb * 128:(mb + 1) * 128],
                          lambda gi, h, ec, mb: mv.t[:, mb, (2 * h + ec) * 128:(2 * h + ec + 1) * 128],
                          lambda gi: [mkT.d, mv.d], False)
                    ckpt('g_xattn')
                    if KD == "x2":
                        for k in range(8):
                            dump_rows(xt.t[:, k, :], xt.d, k * 128, 128, g)
                        continue
                    ffn(xt, G)
                    ckpt('g_ffn')
                    if KD == "x3":
                        for k in range(8):
                            dump_rows(xt.t[:, k, :], xt.d, k * 128, 128, g)
                        continue
                    final_out(xt, G, lambda k, g=g: o_yT[k * 128:(k + 1) * 128, g * G:(g + 1) * G])
        DMA("sp", o_C.rearrange("h d e -> d h e"), Cst.t[:], r=[Cst.d] + dC, sem=Cst.d.sem)
        mo = T("mo", [4, 1], F32, dma=True)
        I("dve", "tensor_add", mo.t[:], Bc.t[:], Mc.t[:], r=[Bc.d, Mc.d], w=[mo.d])
        DMA("sp", o_m, mo.t[:], r=[mo.d], sem=mo.d.sem)


        ckpt("prompt")
        NS = 64

        def carve(ap, name, parents, dma=True):
            t_ = type("CT", (), {})()
            t_.t = ap
            t_.d = P.dep(name, dma=dma)
            rd_ = []
            for pd_ in parents:
                rd_.extend(pd_.readers)
                if pd_.writer is not None:
                    rd_.append(pd_.writer)
            t_.d.readers = rd_
            return t_

        kpar = dka + [kaT.d]
        vpar = dva + [vaa.d]
        kf = kaT.t[:].rearrange("p a b -> p (a b)")
        vfl = vaa.t[:].rearrange("p a b c -> p (a b c)")
        uf = u.t[:].rearrange("p a b -> p (a b)")
        cf = cmask.t[:]
        sK = [carve(kf[:, i * 3600:(i + 1) * 3600].rearrange("p (c n) -> p c n", n=900), "sK%d" % i, kpar) for i in range(2)]
        smk = [carve(kf[:, 7200 + i * 2048:7200 + (i + 1) * 2048].rearrange("p (c n) -> p c n", n=256), "smk%d" % i, kpar)
               for i in range(2)]
        smv = [carve(kf[:, 11296 + i * 2048:11296 + (i + 1) * 2048].rearrange("p (c n) -> p c n", n=1024), "smv%d" % i, kpar)
               for i in range(2)]
        sV = [carve(vfl[:, i * 3584:(i + 1) * 3584].rearrange("p (c n) -> p c n", n=512), "sV%d" % i, vpar) for i in range(2)]
        us = carve(uf[:, 0:896].rearrange("p (k s j) -> p k s j", k=8, s=16), "us", [u.d])
        Cq = [carve(uf[:, 896 + i * 516:896 + (i + 1) * 516].rearrange("p (h e) -> p h e", e=129), "Cq%d" % i, [u.d])
              for i in range(2)]
        cpar = [cmask.d]
        bd = carve(cf[0:32, 0:512], "bd", cpar)
        vaS = carve(cf[0:64, 512:1024], "vaS", cpar)
        sVn = carve(cf[0:4, 1024:1536], "sVn", cpar)
        kaS = carve(cf[:, 1536:1792].rearrange("p (c n) -> p c n", n=64), "kaS", cpar)
        Qbd = carve(cf[:, 1792:1824].rearrange("p (c n) -> p c n", n=8), "Qbd", cpar)
        qmk = [carve(cf[:, 1824 + i * 64:1888 + i * 64], "qmk%d" % i, cpar) for i in range(2)]
        ktm = [carve(cf[0:64, 1952 + i * 128:2080 + i * 128], "ktm%d" % i, cpar) for i in range(2)]
        ktall = carve(cf[0:64, 2224:2736].rearrange("p (h d) -> p h d", d=128), "ktall", cpar)
        rowm = T("rowm", [64, 16], F32, dma=True)
        m0s = T("m0s", [4, 32], F32, dma=True)
        DMA("sp", rowm.t[:], cin["c_rowm"], w=[rowm.d], sem=rowm.d.sem)
        DMA("sp", m0s.t[:, 0:16], s_m0T, w=[m0s.d], sem=m0s.d.sem)
        DMA("pool", bd.t, cin["c_bd"], w=[bd.d], sem=bd.d.sem)

        xt = xpool.next()
        DMA("sp", xt.t[:, :, 0:NS], xsT.rearrange("(k p) n -> p k n", p=128), w=[xt.d], sem=xt.d.sem)
        hT = hpool.next()
        norm(xt, NS, 0, hT)
        for k in range(8):
            DMA("sp", us.t[:, k, :, 0:3], s_convT[k * 128:(k + 1) * 128, :, :], w=[us.d], sem=us.d.sem)

        def ev_us(off):
            def f_(cc, ps, m):
                I("act", "mul", us.t[:, off + cc, :, 3:7], ps.t[:, 0:NS].rearrange("p (s t) -> p s t", t=4), 1.0,
                  r=[ps.d], w=[us.d])
            return f_
        proj_fm("w_in", WIN["qm"], 512, hT, NS, ev_us(0))
        proj_fm("w_in", WIN["km"], 512, hT, NS, ev_us(4))
        for ch in range(8):
            c_ = cv.next()
            cvw = c_.t[:, 0:NS].rearrange("p (s t) -> p s t", t=4)
            I("dve", "tensor_scalar", cvw, us.t[:, ch, :, 0:4], cw.t[:, ch, 0:1], cb.t[:, ch:ch + 1], ALU.mult, ALU.add,
              r=[us.d, cw.d, cb.d], w=[c_.d])
            for j in range(1, 4):
                I("dve", "scalar_tensor_tensor", cvw, us.t[:, ch, :, j:j + 4], cw.t[:, ch, j:j + 1], cvw, ALU.mult, ALU.add,
                  r=[us.d, cw.d, c_.d], w=[c_.d])
            dst = qmT if ch < 4 else kmT
            I("act", "activation", dst.t[:, ch % 4, 0:NS], c_.t[:, 0:NS], AF.Silu, r=[c_.d], w=[dst.d])
        for k in range(8):
            DMA("sp", o_sconvT[k * 128:(k + 1) * 128, :, :], us.t[:, k, :, 4:7], r=[us.d], sem=us.d.sem)

        def ev_vms(tt, ps, m):
            I("act", "mul", vma.t[0:NS, 0, :, 0:128], ps.t[0:NS, 0:512].rearrange("p (h e) -> p h e", e=128), 1.0,
              r=[ps.d], w=[vma.d])
        proj_tm("w_in", WIN["vm"], 512, hT, NS, ev_vms)

        def ev_oms(tt, ps, m):
            I("act", "activation", omS.t[0:NS, 0, :], ps.t[0:NS, 0:512], AF.Sigmoid, r=[ps.d], w=[omS.d])
        proj_tm("w_in", WIN["om"], 512, hT, NS, ev_oms)
        wt = load_w("w_in", WIN["ig"], 8)
        for gi_ in range(2):
            ps = gen.next()
            for k in range(8):
                I("pe", "matmul", ps.t[0:4, 0:NS], wt.t[:, k, 4 * gi_:4 * gi_ + 4], hT.t[:, k, 0:NS],
                  start=(k == 0), stop=(k == 7), r=[wt.d, hT.d], w=[ps.d])
            I("act", "mul", gz.t[:, gi_, 0:NS], ps.t[0:4, 0:NS], 1.0, r=[ps.d], w=[gz.d])
        v3 = lambda ap: ap.rearrange("p (s t) -> p s t", t=4)
        zi = gz.t[0:4, 0, 0:NS]
        zf = gz.t[0:4, 1, 0:NS]
        I("act", "activation", g_e.t[:, 0:NS], zf, AF.Exp, bias=nbf.t[:, 0:1], scale=-1.0, r=[gz.d, nbf.d], w=[g_e.d])
        I("act", "activation", g_e.t[:, 0:NS], g_e.t[:, 0:NS], AF.Ln, bias=one4.t[:, 0:1], r=[g_e.d, one4.d], w=[g_e.d])
        I("dve", "tensor_scalar", g_ws.t[:, 0:NS], g_e.t[:, 0:NS], -1.0, None, ALU.mult, r=[g_e.d], w=[g_ws.d])
        Bv, lv, Gv = v3(g_B.t[:, 0:NS]), v3(g_ws.t[:, 0:NS]), v3(g_G.t[:, 0:NS])
        I("dve", "tensor_copy", Bv[:, :, 0:1], lv[:, :, 0:1], r=[g_ws.d], w=[g_B.d])
        for t in range(1, 4):
            I("dve", "tensor_add", Bv[:, :, t:t + 1], Bv[:, :, t - 1:t], lv[:, :, t:t + 1], r=[g_ws.d, g_B.d], w=[g_B.d])
        I("dve", "tensor_scalar", g_G.t[:, 0:NS], zi, bg.t[:, 0:1], None, ALU.add, r=[gz.d, bg.d], w=[g_G.d])
        I("dve", "tensor_sub", g_G.t[:, 0:NS], g_G.t[:, 0:NS], g_B.t[:, 0:NS], r=[g_G.d, g_B.d], w=[g_G.d])
        Ms = g_e.t[:, 64:80]
        I("dve", "tensor_reduce", Ms, Gv, AX.X, ALU.max, r=[g_G.d, g_e.d], w=[g_e.d])
        I("dve", "tensor_tensor", Ms, Ms, m0s.t[:, 0:16], ALU.max, r=[g_e.d, m0s.d], w=[g_e.d])
        wv, fv = v3(g_ws.t[:, 0:NS]), v3(g_fl.t[:, 0:NS])
        for t in range(4):
            I("dve", "tensor_sub", wv[:, :, t], Gv[:, :, t], Ms, r=[g_G.d, g_e.d, g_ws.d], w=[g_ws.d])
            I("dve", "scalar_tensor_tensor", fv[:, :, t], Bv[:, :, t], -1.0, Ms, ALU.mult, ALU.subtract,
              r=[g_B.d, g_e.d, g_fl.d], w=[g_fl.d])
        I("act", "activation", g_ws.t[:, 0:NS], g_ws.t[:, 0:NS], AF.Exp, r=[g_ws.d], w=[g_ws.d])
        I("act", "activation", g_fl.t[:, 0:NS], g_fl.t[:, 0:NS], AF.Exp, r=[g_fl.d], w=[g_fl.d])
        I("dve", "tensor_sub", g_e.t[:, 80:96], m0s.t[:, 0:16], Ms, r=[m0s.d, g_e.d], w=[g_e.d])
        I("act", "activation", g_e.t[:, 80:96], g_e.t[:, 80:96], AF.Exp, r=[g_e.d], w=[g_e.d])
        I("dve", "tensor_add", m0s.t[:, 16:32], Bv[:, :, 3], Ms, r=[g_B.d, g_e.d, m0s.d], w=[m0s.d])
        DMA("sp", o_sm, m0s.t[:, 16:32], r=[m0s.d], sem=m0s.d.sem)
        for h in range(4):
            ps = gen.next()
            I("pe", "matmul", ps.t[:, 0:NS], selh.t[0:4, h * 128:(h + 1) * 128], g_ws.t[0:4, 0:NS], start=True, stop=True,
              r=[selh.d, g_ws.d], w=[ps.d])
            I("dve", "scalar_tensor_tensor", ktT.t[:, h, 0:NS], ps.t[:, 0:NS], 128.0 ** -0.5, kmT.t[:, h, 0:NS],
              ALU.mult, ALU.mult, r=[ps.d, kmT.d], w=[ktT.d])
        decS = rspool.next()
        ps = gen.next()
        for h in range(4):
            I("pe", "matmul", ps.t[:, 16 * h:16 * h + 16], selh.t[0:4, h * 128:(h + 1) * 128], g_e.t[0:4, 80:96],
              start=True, stop=True, r=[selh.d, g_e.d], w=[ps.d])
        I("dve", "tensor_copy", decS.t[:, 0:64], ps.t[:, 0:64], r=[ps.d], w=[decS.d])
        ps = gen.next()
        I("pe", "matmul", ps.t[0:NS, 0:4], g_fl.t[0:4, 0:NS], id4.t[0:4, 0:4], start=True, stop=True,
          r=[g_fl.d, id4.d], w=[ps.d])
        I("dve", "tensor_copy", floorT.t[0:NS, 0:4], ps.t[0:NS, 0:4], r=[ps.d], w=[floorT.d])
        pNs = []
        for h in range(4):
            pT = psS.next()
            I("pe", "matmul", pT.t[0:NS, 0:128], ktT.t[:, h, 0:NS], ident.t[:], start=True, stop=True,
              r=[ktT.d, ident.d], w=[pT.d])
            I("act", "mul", ktall.t[:, h, :], pT.t[0:NS, 0:128], 1.0, r=[pT.d], w=[ktall.d])
            pS_ = psO.next()
            I("pe", "matmul", pS_.t[0:NS, 0:NS], ktT.t[:, h, 0:NS], qmT.t[:, h, 0:NS], start=True, stop=True,
              r=[ktT.d, qmT.d], w=[pS_.d])
            sd = sdp.next()
            I("dve", "tensor_tensor", sd.t[0:NS, 0:NS], pS_.t[0:NS, 0:NS], tri64.t[:, :], ALU.mult,
              r=[pS_.d, tri64.d], w=[sd.d])
            pN = gen.next()
            I("pe", "matmul", pN.t[0:NS, 0:129], sd.t[0:NS, 0:NS], vma.t[0:NS, 0, h, :], start=True, stop=False,
              r=[sd.d, vma.d], w=[pN.d])
            pNs.append(pN)
        for s_ in range(16):
            cq = Cq[s_ % 2]
            DMA("sp", cq.t[:, :, 0:128], s_C0[s_].rearrange("h d e -> d h e"), w=[cq.d], sem=cq.d.sem)
            DMA("sp", cq.t[:, :, 128], s_n0T[:, s_, :], w=[cq.d], sem=cq.d.sem, allow_slow_non_contiguous=True)
            dq = [P.dep("cq%d_%d" % (s_, h_)) for h_ in range(4)]
            for d_ in dq:
                d_.writer = cq.d.writer
            qks, kms = [], []
            for h in range(4):
                I("dve", "tensor_scalar", cq.t[:, h, :], cq.t[:, h, :], decS.t[:, 16 * h + s_:16 * h + s_ + 1], None, ALU.mult,
                  r=[cq.d, decS.d], w=[dq[h]])
                I("act", "mul", Cb.t[:, h, :], cq.t[:, h, :], 1.0, r=[dq[h]], w=[dCb[h]])
                qk_ = hmp.next()
                I("dve", "tensor_tensor", qk_.t[:, 0:NS], qmT.t[:, h, 0:NS], seqmask.t[:, s_, :], ALU.mult,
                  r=[qmT.d, seqmask.d], w=[qk_.d])
                km_ = ktp.next()
                I("dve", "tensor_scalar", km_.t[0:NS, :], ktall.t[:, h, :], rowm.t[:, s_:s_ + 1], None, ALU.mult,
                  r=[ktall.d, rowm.d], w=[km_.d])
                qks.append(qk_)
                kms.append(km_)
            pUs = []
            for h in range(4):
                I("pe", "matmul", pNs[h].t[0:NS, 0:129], qks[h].t[:, 0:NS], Cb.t[:, h, :], start=False, stop=(s_ == 15),
                  r=[qks[h].d, dCb[h]], w=[pNs[h].d])
                pU = (psS if h % 2 == 0 else psO).next()
                I("pe", "matmul", pU.t[:, 0:129], kms[h].t[0:NS, :], vma.t[0:NS, 0, h, :], start=True, stop=True,
                  r=[kms[h].d, vma.d], w=[pU.d])
                pUs.append(pU)
            for h in range(4):
                I("dve", "tensor_add", cq.t[:, h, :], cq.t[:, h, :], pUs[h].t[:, 0:129], r=[dq[h], pUs[h].d], w=[dq[h]])
            DMA("sp", o_sC[s_].rearrange("h d e -> d h e"), cq.t[:, :, :], r=dq + [cq.d], w=[cq.d], sem=cq.d.sem)
        for h in range(4):
            nh = nhp.next()
            I("act", "mul", nh.t[0:NS, :], pNs[h].t[0:NS, 0:129], 1.0, r=[pNs[h].d], w=[nh.d])
            hb = hmp.next()
            ln_post(NS, nh, floorT.t[0:NS, h:h + 1], floorT.d, gmh.t[0:NS, h * 128:(h + 1) * 128],
                    omS.t[0:NS, 0, h * 128:(h + 1) * 128], omS.d, hb)
            pT = psS.next()
            I("pe", "matmul", pT.t[:, 0:NS], hb.t[0:NS, :], ident.t[0:NS, 0:NS], start=True, stop=True,
              r=[hb.d, ident.d], w=[pT.d])
            I("act", "mul", hmT.t[:, h, 0:NS], pT.t[:, 0:NS], 1.0, r=[pT.d], w=[hmT.d])
        ckpt("s_mlstm")
        def ev_kas(cc, ps, m):
            st = kstg.next()
            I("dve", "tensor_copy", st.t[:, 0:NS], ps.t[:, 0:NS], r=[ps.d], w=[st.d])
            I("act", "mul", kaS.t[:, cc, :], st.t[:, 0:NS], 1.0, r=[st.d], w=[kaS.d])
            DMA("sp", o_skaT[cc * 128:(cc + 1) * 128, :], st.t[:, 0:NS], r=[st.d], sem=st.d.sem)
        proj_fm("w_in", WIN["ka"], 512, hT, NS, ev_kas)

        def ev_vas(tt, ps, m):
            st = kstg.next()
            I("dve", "tensor_copy", st.t[0:NS, :], ps.t[0:NS, 0:512], r=[ps.d], w=[st.d])
            I("act", "mul", vaS.t, st.t[0:NS, :], 1.0, r=[st.d], w=[vaS.d])
            DMA("sp", o_sva, st.t[0:NS, :], r=[st.d], sem=st.d.sem)
        proj_tm("w_in", WIN["va"], 512, hT, NS, ev_vas)

        def ev_qas(cc, ps, m):
            I("act", "mul", qaT.t[:, cc, 0:NS], ps.t[:, 0:NS], 1.0, r=[ps.d], w=[qaT.d])
        proj_fm("w_in", WIN["qa"], 512, hT, NS, ev_qas)
        I("pool", "memset", Qbd.t, 0.0, w=[Qbd.d])
        smf = smask.t[:].rearrange("p a b -> p (a b)")
        for s_ in range(16):
            k_, v_ = sK[s_ % 2], sV[s_ % 2]
            DMA("pool", k_.t[:, :, 0:896], s_kT[s_].rearrange("(c p) n -> p c n", p=128), w=[k_.d], sem=k_.d.sem)
            I("act", "mul", k_.t[:, :, 896:900], kaS.t[:, :, 4 * s_:4 * s_ + 4], 1.0, r=[kaS.d, k_.d], w=[k_.d])
            DMA("pool", v_.t, s_v[s_].rearrange("(t p) c -> p t c", p=128), w=[v_.d], sem=v_.d.sem)
            DMA("sp", sVn.t, vaS.t[4 * s_:4 * s_ + 4, :], r=[vaS.d], w=[sVn.d], sem=sVn.d.sem)
            I("act", "mul", Qbd.t[0:64, :, 0:4], qaT.t[0:64, :, 4 * s_:4 * s_ + 4], 1.0, r=[qaT.d, Qbd.d], w=[Qbd.d])
            I("act", "mul", Qbd.t[64:128, :, 4:8], qaT.t[64:128, :, 4 * s_:4 * s_ + 4], 1.0, r=[qaT.d, Qbd.d], w=[Qbd.d])
            ps = psS.next()
            for kt in range(8):
                for c in range(4):
                    if kt < 7:
                        I("pe", "matmul", ps.t[:, kt * 32 + 8 * c:kt * 32 + 8 * c + 8], k_.t[:, c, kt * 128:(kt + 1) * 128],
                          Qbd.t[:, c, :], start=True, stop=True, r=[k_.d, Qbd.d], w=[ps.d])
                    else:
                        I("pe", "matmul", ps.t[0:4, 224 + 8 * c:232 + 8 * c], k_.t[:, c, 896:900],
                          Qbd.t[:, c, :], start=True, stop=True, r=[k_.d, Qbd.d], w=[ps.d])
            pt = ptp.next()
            I("act", "activation", pt.t[:, 0:224], ps.t[:, 0:224], AF.Exp, scale=0.125, r=[ps.d], w=[pt.d])
            I("act", "activation", pt.t[0:4, 224:256], ps.t[0:4, 224:256], AF.Exp, scale=0.125, r=[ps.d, pt.d], w=[pt.d])
            pm = pmp.next()
            I("dve", "tensor_tensor", pm.t[:, 0:224], pt.t[:, 0:224], smf[:, 0:224], ALU.mult, r=[pt.d, smask.d], w=[pm.d])
            I("dve", "tensor_tensor", pm.t[0:4, 224:256], pt.t[0:4, 224:256], smf[0:4, 224:256], ALU.mult,
              r=[pt.d, smask.d, pm.d], w=[pm.d])
            po = psO.next()
            pd = gen.next()
            for kt in range(8):
                K_ = 128 if kt < 7 else 4
                rhs = v_.t[:, kt, :] if kt < 7 else sVn.t
                rdep = v_.d if kt < 7 else sVn.d
                I("pe", "matmul", po.t[0:32, 0:512], pm.t[0:K_, kt * 32:(kt + 1) * 32], rhs, start=(kt == 0), stop=(kt == 7),
                  r=[pm.d, rdep], w=[po.d])
            for kt in range(8):
                K_ = 128 if kt < 7 else 4
                I("pe", "matmul", pd.t[0:32, 0:1], pm.t[0:K_, kt * 32:(kt + 1) * 32], ones1.t[0:K_, 0:1],
                  start=(kt == 0), stop=(kt == 7), r=[pm.d, ones1.d], w=[pd.d])
            ot = kstg.next()
            I("dve", "tensor_tensor", ot.t[0:32, :], po.t[0:32, 0:512], bd.t, ALU.mult, r=[po.d, bd.d], w=[ot.d])
            od = hrp.next()
            I("dve", "tensor_reduce", od.t[0:32, 0:64], ot.t[0:32, :].rearrange("p (h e) -> p e h", e=64), AX.X, ALU.add,
              r=[ot.d], w=[od.d])
            stq = stp.next()
            I("dve", "reciprocal", stq.t[0:32, 0:1], pd.t[0:32, 0:1], r=[pd.d], w=[stq.d])
            odn = hmp.next()
            I("dve", "tensor_scalar", odn.t[0:32, 0:64], od.t[0:32, 0:64], stq.t[0:32, 0:1], None, ALU.mult,
              r=[od.d, stq.d], w=[odn.d])
            pT = psS.next()
            I("pe", "matmul", pT.t[0:64, 0:32], odn.t[0:32, 0:64], ident.t[0:32, 0:32], start=True, stop=True,
              r=[odn.d, ident.d], w=[pT.d])
            I("act", "mul", haT.t[:, :, 4 * s_:4 * s_ + 4], pT.t[0:64, 0:32].rearrange("p (h t) -> p h t", t=4), 1.0,
              r=[pT.d], w=[haT.d])
        ckpt("s_attn")
        srcs = [("w", k, hmT.t[:, k, 0:NS], hmT.d) for k in range(4)] + \
               [("a", h, haT.t[:, h, 0:NS], haT.d) for h in range(8)]
        resid_proj("w_out", xt, NS, srcs, att=True)

        def pre_mem(gi):
            DMA("pool", smk[gi % 2].t, s_mkT[gi].rearrange("(k p) m -> p k m", p=128), w=[smk[gi % 2].d],
                sem=smk[gi % 2].d.sem)
            DMA("pool", smv[gi % 2].t, s_mv[gi].rearrange("(mb p) c -> p mb c", p=128), w=[smv[gi % 2].d],
                sem=smv[gi % 2].d.sem)
        xattn(xt, NS,
              lambda gi, h, ec, mb: smk[gi % 2].t[:, 2 * h + ec, mb * 128:(mb + 1) * 128],
              lambda gi, h, ec, mb: smv[gi % 2].t[:, mb, (2 * h + ec) * 128:(2 * h + ec + 1) * 128],
              lambda gi: [smk[gi % 2].d, smv[gi % 2].d], True, pre=pre_mem)
        ffn(xt, NS)
        final_out(xt, NS, lambda k: o_ysT[k * 128:(k + 1) * 128, :])
    except _Stop:
        pass
    P.emit()
    P.run(E)
    es.close()
    return nc


_NC = None


def _get_nc():
    global _NC
    if _NC is None:
        _NC = build()
    return _NC


def make_in_maps(x_prompt, x_sample, mem_prompt, cache_attn_k, cache_attn_v, cache_mem_k, cache_mem_v,
           state_conv, state_C, state_n, state_m, g_mix, w_in, conv_w, conv_b, b_gates, g_mh, w_out,
           g_mem, w_mk, w_mv, g_xattn, w_mq, w_mo, g_ffn, w_gate, w_up, w_down, g_final, cores=None):
    f = lambda a: np.ascontiguousarray(np.asarray(a, dtype=np.float32))
    consts = host_consts()
    pos = _sel_positions()
    gvec = f(np.stack([g_mix, g_mem, g_xattn, g_ffn, g_final]))
    shared = dict(w_in=f(w_in), w_out=f(w_out), w_mk=f(w_mk), w_mv=f(w_mv), w_mq=f(w_mq), w_mo=f(w_mo),
                  w_gate=f(w_gate), w_up=f(w_up), w_down=f(w_down), gvec=gvec, conv_w=f(conv_w), conv_b=f(conv_b),
                  b_gates=f(b_gates), g_mh=f(g_mh))
    shared.update(consts)
    x_prompt = np.asarray(x_prompt, np.float32)
    in_maps = []
    for c in (range(NCORES) if cores is None else cores):
        b, j = c // 4, c % 4
        m = dict(shared)
        xb = np.zeros((4, 1024, 2048), np.float32)
        fl = np.zeros((4, 12), np.float32)
        for i in range(4):
            bi = j - 3 + i
            if bi >= 0:
                xb[i] = x_prompt[b, bi * 2048:(bi + 1) * 2048, :].T
                fl[:, i] = 1.0
                fl[:, 8 + i] = -1.0
            else:
                fl[:, 4 + i] = -1e4
        m["xblk"] = xb
        m["flags"] = fl
        m["hflag"] = np.full((128, 1), 1.0 if j >= 1 else 0.0, np.float32)
        sl = slice(16 * c, 16 * c + 16)
        m["xsT"] = f(np.asarray(x_sample)[sl].reshape(64, 1024).T)
        m["memT"] = f(np.asarray(mem_prompt)[b].T)
        ck = np.asarray(cache_attn_k)[sl][:, pos].reshape(16, NSEL, 512)
        m["s_kT"] = f(ck.transpose(0, 2, 1))
        m["s_v"] = f(np.asarray(cache_attn_v)[sl][:, pos].reshape(16, NSEL, 512))
        m["s_mkT"] = f(np.asarray(cache_mem_k)[sl].reshape(16, 256, 1024).transpose(0, 2, 1))
        m["s_mv"] = f(np.asarray(cache_mem_v)[sl].reshape(16, 256, 1024))
        m["s_convT"] = f(np.asarray(state_conv)[sl].transpose(2, 0, 1))
        m["s_C0"] = f(np.asarray(state_C)[sl])
        m["s_n0T"] = f(np.asarray(state_n)[sl].transpose(2, 0, 1))
        m["s_m0T"] = f(np.asarray(state_m)[sl].T)
        in_maps.append(m)
    return in_maps


def kernel(**inputs):
    in_maps = make_in_maps(**inputs)
    nc = _get_nc()
    res = run_bass_kernel_spmd(nc, in_maps, core_ids=list(range(NCORES)))
    return assemble(res.results)


def assemble(R):
    y_prompt = np.zeros((2, 8192, 1024), np.float32)
    for c in range(NCORES):
        b, j = c // 4, c % 4
        y_prompt[b, j * 2048:(j + 1) * 2048] = R[c]["o_yT"].T
    y_sample = np.concatenate([R[c]["o_ysT"].T.reshape(16, 4, 1024) for c in range(NCORES)], 0)
    last = [3, 7]
    p_attn_k = np.stack([R[c]["o_kaT"].T.reshape(2048, 8, 64) for c in last])
    p_attn_v = np.stack([R[c]["o_va"].reshape(2048, 8, 64) for c in last])
    p_conv = np.stack([R[c]["o_convT"].T for c in last])
    p_C = np.stack([R[c]["o_C"][:, :, 0:128] for c in last])
    p_n = np.stack([R[c]["o_C"][:, :, 128] for c in last])
    p_m = np.stack([R[c]["o_m"][:, 0] for c in last])
    p_mem_k = np.stack([R[c]["o_mkT"].T.reshape(256, 4, 256) for c in (0, 4)])
    p_mem_v = np.stack([R[c]["o_mv"].reshape(256, 4, 256) for c in (0, 4)])
    s_attn_k = np.concatenate([R[c]["o_skaT"].T.reshape(16, 4, 8, 64) for c in range(NCORES)], 0)
    s_attn_v = np.concatenate([R[c]["o_sva"].reshape(16, 4, 8, 64) for c in range(NCORES)], 0)
    s_conv = np.concatenate([R[c]["o_sconvT"].transpose(1, 2, 0) for c in range(NCORES)], 0)
    s_C = np.concatenate([R[c]["o_sC"][:, :, :, 0:128] for c in range(NCORES)], 0)
    s_n = np.concatenate([R[c]["o_sC"][:, :, :, 128] for c in range(NCORES)], 0)
    s_m = np.concatenate([R[c]["o_sm"].T for c in range(NCORES)], 0)
    outs = (y_prompt, y_sample, p_attn_k, p_attn_v, p_conv, p_C, p_n, p_m, p_mem_k, p_mem_v,
            s_attn_k, s_attn_v, s_conv, s_C, s_n, s_m)
    return tuple(np.ascontiguousarray(o, dtype=np.float32) for o in outs)
```

```python
import numpy as np
from contextlib import ExitStack
import concourse.bass as bass
import concourse.mybir as mybir
from concourse.bass_utils import run_bass_kernel_spmd

F32 = mybir.dt.float32
BF16 = mybir.dt.bfloat16
AF = mybir.ActivationFunctionType
ALU = mybir.AluOpType
AX = mybir.AxisListType

import os as _osx
SAME_ENGINE_SYNC = _osx.environ.get('KSES', 'raw')
NCORES = 8
WIN = dict(qm=0, km=512, vm=1024, om=1536, ig=2048, fg=2052, qa=2056, ka=2568, va=3080)
D_FF = 2816
NSEL = 896
G = 256
NG = 2048 // G
GT = G // 128


class Dep:
    __slots__ = ("name", "writer", "readers", "sem")

    def __init__(self, name, sem=None):
        self.name = name
        self.writer = None
        self.readers = []
        self.sem = sem


class DmaSem:
    __slots__ = ("name", "count", "handle")

    def __init__(self, name):
        self.name = name
        self.count = 0
        self.handle = None


class Op:
    __slots__ = ("eng", "fn", "deps", "needs_signal", "token", "dsem", "waits")


class Prog:
    ENGS = ("pe", "act", "dve", "pool", "sp")

    def __init__(self, nc):
        self.nc = nc
        self.ops = {e: [] for e in self.ENGS}
        self.dsems = []
        self.seq = []

    def dma_sem(self, name):
        s = DmaSem(name)
        self.dsems.append(s)
        return s

    def dep(self, name, dma=False):
        return Dep(name, self.dma_sem(name) if dma else None)

    def op(self, eng, fn, reads=(), writes=(), dsem=None):
        o = Op()
        o.eng = eng
        o.fn = fn
        o.dsem = dsem
        o.needs_signal = dsem is not None
        o.token = None
        deps = {}
        raw = set()
        for d in reads:
            if d.writer is not None:
                deps[id(d.writer)] = d.writer
                raw.add(id(d.writer))
        for d in writes:
            if d.writer is not None:
                deps[id(d.writer)] = d.writer
            for r in d.readers:
                deps[id(r)] = r
        dl = []
        for x in deps.values():
            if x is o:
                continue
            if x.eng == eng and x.dsem is None:
                if eng == "pe" or not SAME_ENGINE_SYNC:
                    continue
                if SAME_ENGINE_SYNC == "raw" and id(x) not in raw:
                    continue
            dl.append(x)
        o.deps = dl
        for x in dl:
            x.needs_signal = True
        for d in reads:
            d.readers.append(o)
        for d in writes:
            d.writer = o
            d.readers = []
        self.ops[eng].append(o)
        self.seq.append(o)
        return o

    def emit(self):
        for o in self.seq:
            if o.dsem is not None:
                o.dsem.count += 16
                o.token = (o.dsem, o.dsem.count)
        for e in self.ENGS:
            cnt = 0
            for o in self.ops[e]:
                if o.dsem is None and o.needs_signal:
                    cnt += 1
                    o.token = (e, cnt)
        for e in self.ENGS:
            seen = {}
            for o in self.ops[e]:
                w = {}
                for x in o.deps:
                    k, v = x.token
                    kk = id(k) if isinstance(k, DmaSem) else k
                    if seen.get(kk, 0) >= v:
                        continue
                    if kk not in w or w[kk][1] < v:
                        w[kk] = (k, v)
                for kk, (k, v) in w.items():
                    seen[kk] = v
                o.waits = list(w.values())

    def run(self, E):
        nc = self.nc
        esem = {e: E(nc.semaphore("s_" + e)) for e in self.ENGS}
        for i, s in enumerate(self.dsems):
            if s.count > 0:
                s.handle = E(nc.semaphore("d%d_%s" % (i, s.name)))
        block = E(nc.Block())

        def replay(eng, key):
            for o in self.ops[key]:
                for (k, v) in o.waits:
                    eng.wait_ge(k.handle if isinstance(k, DmaSem) else esem[k], v)
                ins = o.fn(eng)
                if o.dsem is not None:
                    ins.then_inc(o.dsem.handle, 16)
                elif o.needs_signal:
                    ins.then_inc(esem[key], 1)
            if key == "sp":
                for s in self.dsems:
                    if s.count > 0:
                        eng.wait_ge(s.handle, s.count)

        @block.tensor
        def _(eng):
            replay(eng, "pe")

        @block.scalar
        def _(eng):
            replay(eng, "act")

        @block.vector
        def _(eng):
            replay(eng, "dve")

        @block.gpsimd
        def _(eng):
            replay(eng, "pool")

        @block.sync
        def _(eng):
            replay(eng, "sp")


def _cfun(d):
    d = np.asarray(d)
    c = ((d >= 0) & (d <= 128)).astype(np.float32)
    c += ((d >= 0) & (d % 4 == 0) & (d <= 512)).astype(np.float32)
    c += ((d >= 0) & (d % 16 == 0) & (d <= 2048)).astype(np.float32)
    return c


def _sel_positions():
    p = np.arange(2048)
    keep = (p >= 1536) | (p % 16 < 4)
    return p[keep]


def host_consts():
    c = {}
    ik = np.arange(128)[:, None]
    u = np.arange(2944)[None, :] - 384
    c["c_cmask"] = _cfun(u - ik).astype(np.float32)
    s = np.arange(128)
    c["c_tri"] = (s[:, None] <= s[None, :]).astype(np.float32)
    sel = np.zeros((4, 4, 128), np.float32)
    for h in range(4):
        sel[h, h, :] = 1.0
    c["c_selh"] = sel.reshape(4, 512)
    c["c_id4"] = np.eye(4, dtype=np.float32)
    pos = _sel_positions()
    assert len(pos) == NSEL
    sm = np.zeros((128, 8, 32), np.float32)
    for kt in range(8):
        for r in range(128):
            if kt < 7:
                p = pos[kt * 128 + r]
            elif r < 4:
                p = 2048 + r
            else:
                continue
            for t in range(4):
                sm[r, kt, t::4] = _cfun(2048 + t - p)
    c["c_smask"] = sm
    i = np.arange(64)
    c["c_tri64"] = ((i[:, None] // 4 == i[None, :] // 4) & (i[:, None] <= i[None, :])).astype(np.float32)
    qm = np.zeros((128, 16, 64), np.float32)
    for sq in range(16):
        qm[:, sq, 4 * sq:4 * sq + 4] = 1.0
    c["c_seqmask"] = qm
    p32 = np.arange(32)[:, None]
    c["c_bd"] = (p32 // 4 == (np.arange(512)[None, :] // 64)).astype(np.float32)
    c["c_rowm"] = (np.arange(64)[:, None] // 4 == np.arange(16)[None, :]).astype(np.float32)
    return c


def build():
    nc = bass.Bass("TRN2", target_bir_lowering=False)
    P = Prog(nc)
    es = ExitStack()
    E = es.enter_context

    def din(name, shape):
        return nc.dram_tensor(name, list(shape), F32, kind="ExternalInput").ap()

    def dout(name, shape):
        return nc.dram_tensor(name, list(shape), F32, kind="ExternalOutput").ap()

    xblk = din("xblk", [4, 1024, 2048])
    xsT = din("xsT", [1024, 64])
    memT = din("memT", [1024, 256])
    flags = din("flags", [4, 12])
    hflag = din("hflag", [128, 1])
    Wd = {}
    for nm, sh in (("w_in", [1024, 3592]), ("w_out", [1024, 1024]), ("w_mk", [1024, 1024]), ("w_mv", [1024, 1024]),
                   ("w_mq", [1024, 1024]), ("w_mo", [1024, 1024]), ("w_gate", [1024, D_FF]), ("w_up", [1024, D_FF]),
                   ("w_down", [D_FF, 1024])):
        Wd[nm] = din(nm, sh)
    gvec = din("gvec", [5, 1024])
    conv_w = din("conv_w", [4, 1024])
    conv_b = din("conv_b", [1024])
    b_gates = din("b_gates", [8])
    g_mh = din("g_mh", [512])
    cin = {k: din(k, v.shape) for k, v in host_consts().items()}
    s_kT = din("s_kT", [16, 512, NSEL])
    s_v = din("s_v", [16, NSEL, 512])
    s_mkT = din("s_mkT", [16, 1024, 256])
    s_mv = din("s_mv", [16, 256, 1024])
    s_convT = din("s_convT", [1024, 16, 3])
    s_C0 = din("s_C0", [16, 4, 128, 128])
    s_n0T = din("s_n0T", [128, 16, 4])
    s_m0T = din("s_m0T", [4, 16])

    o_yT = dout("o_yT", [1024, 2048])
    o_ysT = dout("o_ysT", [1024, 64])
    o_kaT = dout("o_kaT", [512, 2048])
    o_va = dout("o_va", [2048, 512])
    o_convT = dout("o_convT", [1024, 3])
    o_C = dout("o_C", [4, 128, 129])
    o_m = dout("o_m", [4, 1])
    o_mkT = dout("o_mkT", [1024, 256])
    o_mv = dout("o_mv", [256, 1024])
    o_skaT = dout("o_skaT", [512, 64])
    o_sva = dout("o_sva", [64, 512])
    o_sconvT = dout("o_sconvT", [1024, 16, 3])
    o_sC = dout("o_sC", [16, 4, 128, 129])
    o_sm = dout("o_sm", [4, 16])

    Wb = {}
    dWb = {}
    for nm, ap in Wd.items():
        t = nc.dram_tensor(nm + "_bf", list(ap.shape), BF16)
        Wb[nm] = t.ap()
        dWb[nm] = P.dep("wb_" + nm, dma=True)

    class T:
        def __init__(self, name, shape, dt=F32, psum=False, dma=False):
            self.t = E(nc.psum_tensor(name, list(shape), dt) if psum else nc.sbuf_tensor(name, list(shape), dt))
            self.d = P.dep(name, dma=dma)

    class Pool:
        def __init__(self, name, n, shape, dt=F32, psum=False, dma=False):
            self.tiles = [T("%s%d" % (name, i), shape, dt, psum, dma) for i in range(n)]
            self.i = 0

        def next(self):
            t = self.tiles[self.i % len(self.tiles)]
            self.i += 1
            return t

    def I(eng, meth, *a, r=(), w=(), **kw):
        return P.op(eng, lambda e: getattr(e, meth)(*a, **kw), r, w)

    def DMA(q, out, in_, r=(), w=(), sem=None, **kw):
        return P.op(q, lambda e: e.dma_start(out=out, in_=in_, **kw), r, w, dsem=sem)

    gen = Pool("psg", 4, [128, 512], F32, psum=True)
    psS = Pool("pss", 2, [128, 512], F32, psum=True)
    psO = Pool("pso", 2, [128, 512], F32, psum=True)

    ones_s = T("ones_s", [128, 128], BF16)
    ident = T("ident", [128, 128], BF16)
    onesf = T("onesf", [128, 128], F32)
    eps_t = T("eps_t", [128, 1], F32)
    gv = T("gv", [128, 5, 8], F32, dma=True)
    cw = T("cw", [128, 8, 4], F32, dma=True)
    cb = T("cb", [128, 8], F32, dma=True)
    gmh = T("gmh", [128, 512], F32, dma=True)
    bg = T("bg", [4, 2], F32, dma=True)
    nbf = T("nbf", [4, 1], F32)
    fl4 = T("fl4", [4, 12], F32, dma=True)
    hfl = T("hfl", [128, 1], F32, dma=True)
    tri = T("tri", [128, 128], BF16, dma=True)
    selh = T("selh", [4, 512], F32, dma=True)
    id4 = T("id4", [4, 4], F32, dma=True)
    cmask = T("cmask", [128, 2944], BF16, dma=True)
    smask = T("smask", [128, 8, 32], F32, dma=True)
    tri64 = T("tri64", [64, 64], F32, dma=True)
    seqmask = T("seqmask", [128, 16, 64], BF16, dma=True)

    I("pool", "memset", ones_s.t[:], 1.0 / 1024.0, w=[ones_s.d])
    I("pool", "memset", onesf.t[:], 1.0, w=[onesf.d])
    I("pool", "memset", eps_t.t[:], 1e-6, w=[eps_t.d])
    I("pool", "memset", ident.t[:], 1.0, w=[ident.d])
    I("pool", "affine_select", ident.t[:], ident.t[:], [[-1, 128]], ALU.is_equal, 0.0, base=0, channel_multiplier=1,
      r=[ident.d], w=[ident.d])
    DMA("sp", gv.t[:], gvec.rearrange("g (k p) -> p g k", p=128), w=[gv.d], sem=gv.d.sem, allow_slow_non_contiguous=True)
    for j_ in range(4):
        DMA("sp", cw.t[:, :, j_], conv_w[j_].rearrange("(k p) -> p k", p=128), w=[cw.d], sem=cw.d.sem,
            allow_slow_non_contiguous=True)
    DMA("sp", cb.t[:], conv_b.rearrange("(k p) -> p k", p=128), w=[cb.d], sem=cb.d.sem, allow_slow_non_contiguous=True)
    DMA("sp", gmh.t[:], g_mh.partition_broadcast(128), w=[gmh.d], sem=gmh.d.sem)
    DMA("sp", bg.t[:], b_gates.rearrange("(two h) -> h two", two=2), w=[bg.d], sem=bg.d.sem, allow_slow_non_contiguous=True)
    DMA("sp", fl4.t[:], flags, w=[fl4.d], sem=fl4.d.sem)
    DMA("sp", hfl.t[:], hflag, w=[hfl.d], sem=hfl.d.sem)
    DMA("sp", selh.t[:], cin["c_selh"], w=[selh.d], sem=selh.d.sem)
    DMA("sp", id4.t[:], cin["c_id4"], w=[id4.d], sem=id4.d.sem)
    DMA("sp", smask.t[:], cin["c_smask"], w=[smask.d], sem=smask.d.sem)
    DMA("sp", tri64.t[:], cin["c_tri64"], w=[tri64.d], sem=tri64.d.sem)
    DMA("pool", tri.t[:], cin["c_tri"], w=[tri.d], sem=tri.d.sem)
    DMA("pool", cmask.t[:], cin["c_cmask"], w=[cmask.d], sem=cmask.d.sem)
    DMA("pool", seqmask.t[:], cin["c_seqmask"], w=[seqmask.d], sem=seqmask.d.sem)
    I("dve", "tensor_scalar", nbf.t[:], bg.t[:, 1:2], -1.0, None, ALU.mult, r=[bg.d], w=[nbf.d])


    wpool = Pool("wt", 3, [128, 8, 512], BF16, dma=True)
    wdpool = Pool("wdn", 3, [128, 11, 128], BF16, dma=True)
    xpool = Pool("xg", 1, [128, 8, G], F32, dma=True)
    sqpool = Pool("sq", 2, [128, G], BF16)
    rspool = Pool("rs", 2, [128, G], F32)
    hpool = Pool("hT", 2, [128, 8, G], BF16)

    def load_w(nm, c0, ncols):
        wt = wpool.next()
        DMA("sp", wt.t[:, :, 0:ncols], Wb[nm][:, c0:c0 + ncols].rearrange("(k p) c -> p k c", p=128),
            r=[dWb[nm]], w=[wt.d], sem=wt.d.sem)
        return wt

    import os as _os2
    def precast(nm, only=None, skip=()):
        src, dst = Wd[nm], Wb[nm]
        rows, cols = src.shape
        for r0 in range(0, rows, 1024):
            kk = min(8, (rows - r0) // 128)
            for c0 in range(0, cols, 512):
                if (only is not None and c0 not in only) or c0 in skip:
                    continue
                cw_ = min(512, cols - c0)
                st = wpool.next()
                DMA("pool", st.t[:, 0:kk, 0:cw_], src[r0:r0 + kk * 128, c0:c0 + cw_].rearrange("(k p) c -> p k c", p=128),
                    w=[st.d], sem=st.d.sem)
                DMA("pool", dst[r0:r0 + kk * 128, c0:c0 + cw_].rearrange("(k p) c -> p k c", p=128), st.t[:, 0:kk, 0:cw_],
                    r=[st.d], w=[dWb[nm]], sem=st.d.sem)

    def norm(xt, n, gi, hT):
        ps = gen.next()
        for k in range(8):
            sq = sqpool.next()
            I("act", "activation", sq.t[:, 0:n], xt.t[:, k, 0:n], AF.Square, r=[xt.d], w=[sq.d])
            I("pe", "matmul", ps.t[:, 0:n], ones_s.t[:], sq.t[:, 0:n], start=(k == 0), stop=(k == 7),
              r=[ones_s.d, sq.d], w=[ps.d])
        rs = rspool.next()
        I("act", "activation", rs.t[:, 0:n], ps.t[:, 0:n], AF.Ln, bias=eps_t.t[:, 0:1], r=[ps.d, eps_t.d], w=[rs.d])
        I("act", "activation", rs.t[:, 0:n], rs.t[:, 0:n], AF.Exp, scale=-0.5, r=[rs.d], w=[rs.d])
        if hT is None:
            return rs
        for k in range(8):
            I("dve", "scalar_tensor_tensor", hT.t[:, k, 0:n], xt.t[:, k, 0:n], gv.t[:, gi, k:k + 1], rs.t[:, 0:n],
              ALU.mult, ALU.mult, r=[xt.d, gv.d, rs.d], w=[hT.d])

    def proj_fm(nm, c0, ncols, hT, n, evac, K=8):
        wt = load_w(nm, c0, ncols)
        ckpt("lw")
        for cc in range((ncols + 127) // 128):
            m = min(128, ncols - cc * 128)
            ps = gen.next()
            for k in range(K):
                I("pe", "matmul", ps.t[0:m, 0:n], wt.t[:, k, cc * 128:cc * 128 + m], hT.t[:, k, 0:n],
                  start=(k == 0), stop=(k == K - 1), r=[wt.d, hT.d], w=[ps.d])
            ckpt("mm")
            evac(cc, ps, m)
            ckpt("ev")

    def proj_tm(nm, c0, ncols, hT, n, evac):
        wt = load_w(nm, c0, ncols)
        for tt in range((n + 127) // 128):
            m = min(128, n - tt * 128)
            ps = gen.next()
            for k in range(8):
                I("pe", "matmul", ps.t[0:m, 0:ncols], hT.t[:, k, tt * 128:tt * 128 + m], wt.t[:, k, 0:ncols],
                  start=(k == 0), stop=(k == 7), r=[wt.d, hT.d], w=[ps.d])
            evac(tt, ps, m)

    kstg = Pool("kstg", 2, [128, 512], F32, dma=True)

    class _Stop(Exception):
        pass

    def ckpt(name):
        if _os2.environ.get("KSTOP") == name:
            raise _Stop()

    try:
        mkT = T("mkT", [128, 8, 256], BF16)
        mv = T("mv", [128, 2, 1024], BF16)
        def mem_phase():
            precast("w_mk")
            precast("w_mv")
            xm = xpool.next()
            DMA("sp", xm.t[:, :, 0:256], memT.rearrange("(k p) n -> p k n", p=128), w=[xm.d], sem=xm.d.sem)
            hm_ = hpool.next()
            norm(xm, 256, 1, hm_)
            for half in range(2):
                def ev_mk(cc, ps, m, half=half):
                    ch = half * 4 + cc
                    st = kstg.next()
                    I("dve", "tensor_copy", st.t[:, 0:256], ps.t[:, 0:256], r=[ps.d], w=[st.d])
                    I("act", "mul", mkT.t[:, ch, :], st.t[:, 0:256], 1.0, r=[st.d], w=[mkT.d])
                    DMA("sp", o_mkT[ch * 128:(ch + 1) * 128, :], st.t[:, 0:256], r=[st.d], sem=st.d.sem)
                proj_fm("w_mk", half * 512, 512, hm_, 256, ev_mk)
                ckpt("mk%d" % half)
            for half in range(2):
                def ev_mv(tt, ps, m, half=half):
                    st = kstg.next()
                    I("dve", "tensor_copy", st.t[:, 0:512], ps.t[:, 0:512], r=[ps.d], w=[st.d])
                    I("act", "mul", mv.t[:, tt, half * 512:(half + 1) * 512], st.t[:, 0:512], 1.0, r=[st.d], w=[mv.d])
                    DMA("sp", o_mv[tt * 128:(tt + 1) * 128, half * 512:(half + 1) * 512], st.t[:, 0:512],
                        r=[st.d], sem=st.d.sem)
                proj_tm("w_mv", half * 512, 512, hm_, 256, ev_mv)
                ckpt("mv%d" % half)


        ckpt("mem")
        kaT = T("kaT", [128, 4, 4096], BF16)
        vaa = T("vaa", [128, 32, 8, 65], BF16)
        I("pool", "memset", vaa.t[:], 1.0, w=[vaa.d])
        I("dve", "tensor_scalar", vaa.t[:, 0:16, :, 64:65], vaa.t[:, 0:16, :, 64:65], hfl.t[:, 0:1], None, ALU.mult,
          r=[vaa.d, hfl.d], w=[vaa.d])
        dka = [P.dep("ka%d" % i) for i in range(2 * NG)]
        dva = [P.dep("va%d" % i) for i in range(2 * NG)]
        for d_ in dva:
            d_.writer = vaa.d.writer
        u = T("u", [128, 8, G + 3], F32, dma=True)
        I("pool", "memset", u.t[:], 0.0, w=[u.d])
        Cst = T("Cst", [128, 4, 129], F32, dma=True)
        Cb = T("Cb", [128, 4, 129], BF16)
        dC = [P.dep("dC%d" % h_) for h_ in range(4)]
        dCb = [P.dep("dCb%d" % h_) for h_ in range(4)]
        I("pool", "memset", Cst.t[:], 0.0, w=[Cst.d] + dC)
        Bc = T("Bc", [4, 1], F32)
        Mc = T("Mc", [4, 1], F32)
        I("pool", "memset", Bc.t[:], 0.0, w=[Bc.d])
        I("pool", "memset", Mc.t[:], 0.0, w=[Mc.d])
        one4 = T("one4", [4, 1], F32)
        I("pool", "memset", one4.t[:], 1.0, w=[one4.d])

        qmT = T("qmT", [128, 4, G], BF16)
        kmT = T("kmT", [128, 4, G], BF16)
        ktT = T("ktT", [128, 4, G], BF16)
        vma = T("vma", [128, GT, 4, 129], BF16)
        I("pool", "memset", vma.t[:], 1.0, w=[vma.d])
        omS = T("omS", [128, GT, 512], BF16)
        cv = Pool("cv", 2, [128, G], F32)
        gz = T("gz", [4, 2, G], F32)
        g_e = T("g_e", [4, G], F32)
        g_B = T("g_B", [4, G], F32)
        g_G = T("g_G", [4, G], F32)
        g_ws = T("g_ws", [4, G], F32)
        g_fl = T("g_fl", [4, G], F32)
        g_s = T("g_s", [4, 24], F32)
        decB = T("decB", [128, 16], F32)
        floorT = T("floorT", [128, 16], F32)
        ktp = Pool("kt", 4, [128, 128], BF16)
        sdp = Pool("sd", 2, [128, 128], BF16)
        nhp = Pool("nh", 4, [128, 129], F32)
        hrp = Pool("hr", 2, [128, 128], F32)
        sqp = Pool("sq2", 2, [128, 128], F32)
        stp = Pool("stt", 2, [128, 12], F32)
        hmp = Pool("hmb", 4, [128, 128], BF16)
        hmT = T("hmT", [128, 4, G], BF16)

        def gates(n, blk, full):
            nch = n // 128
            zi = gz.t[0:4, 0, 0:n]
            zf = gz.t[0:4, 1, 0:n]
            I("act", "activation", g_e.t[:, 0:n], zf, AF.Exp, bias=nbf.t[:, 0:1], scale=-1.0, r=[gz.d, nbf.d], w=[g_e.d])
            I("act", "activation", g_e.t[:, 0:n], g_e.t[:, 0:n], AF.Ln, bias=one4.t[:, 0:1], r=[g_e.d, one4.d], w=[g_e.d])
            I("dve", "tensor_scalar", g_e.t[:, 0:n], g_e.t[:, 0:n], fl4.t[:, 8 + blk:9 + blk], None, ALU.mult,
              r=[g_e.d, fl4.d], w=[g_e.d])
            I("dve", "tensor_tensor_scan", g_B.t[:, 0:n], g_e.t[:, 0:n], g_e.t[:, 0:n], Bc.t[:, 0:1], ALU.add, ALU.bypass,
              r=[g_e.d, Bc.d], w=[g_B.d])
            I("dve", "tensor_scalar", g_G.t[:, 0:n], zi, bg.t[:, 0:1], None, ALU.add, r=[gz.d, bg.d], w=[g_G.d])
            I("dve", "tensor_scalar", g_G.t[:, 0:n], g_G.t[:, 0:n], fl4.t[:, blk:blk + 1], fl4.t[:, 4 + blk:5 + blk],
              ALU.mult, ALU.add, r=[g_G.d, fl4.d], w=[g_G.d])
            I("dve", "tensor_sub", g_G.t[:, 0:n], g_G.t[:, 0:n], g_B.t[:, 0:n], r=[g_G.d, g_B.d], w=[g_G.d])
            I("dve", "tensor_reduce", g_s.t[:, 0:nch], g_G.t[:, 0:n].rearrange("p (c t) -> p c t", t=128), AX.X, ALU.max,
              r=[g_G.d], w=[g_s.d])
            I("dve", "tensor_tensor_scan", g_s.t[:, 4:4 + nch], g_s.t[:, 0:nch], g_s.t[:, 0:nch], Mc.t[:, 0:1], ALU.max,
              ALU.bypass, r=[g_s.d, Mc.d], w=[g_s.d])
            I("dve", "tensor_copy", g_s.t[:, 8:9], Mc.t[:, 0:1], r=[Mc.d, g_s.d], w=[g_s.d])
            if nch > 1:
                I("dve", "tensor_copy", g_s.t[:, 9:8 + nch], g_s.t[:, 4:3 + nch], r=[g_s.d], w=[g_s.d])
            I("dve", "tensor_scalar", g_s.t[:, 12:12 + nch], g_s.t[:, 4:4 + nch], -1.0, None, ALU.mult, r=[g_s.d], w=[g_s.d])
            I("dve", "tensor_sub", g_s.t[:, 16:16 + nch], g_s.t[:, 8:8 + nch], g_s.t[:, 4:4 + nch], r=[g_s.d], w=[g_s.d])
            I("act", "activation", g_s.t[:, 20:20 + nch], g_s.t[:, 16:16 + nch], AF.Exp, r=[g_s.d], w=[g_s.d])
            for c in range(nch):
                I("act", "activation", g_ws.t[:, c * 128:(c + 1) * 128], g_G.t[:, c * 128:(c + 1) * 128], AF.Exp,
                  bias=g_s.t[:, 12 + c:13 + c], r=[g_G.d, g_s.d], w=[g_ws.d])
                if full:
                    I("act", "activation", g_fl.t[:, c * 128:(c + 1) * 128], g_B.t[:, c * 128:(c + 1) * 128], AF.Exp,
                      bias=g_s.t[:, 12 + c:13 + c], scale=-1.0, r=[g_B.d, g_s.d], w=[g_fl.d])
            I("dve", "tensor_copy", Bc.t[:, 0:1], g_B.t[:, n - 1:n], r=[g_B.d], w=[Bc.d])
            I("dve", "tensor_copy", Mc.t[:, 0:1], g_s.t[:, 3 + nch:4 + nch], r=[g_s.d], w=[Mc.d])
            for h in range(4):
                ps = gen.next()
                I("pe", "matmul", ps.t[:, 0:n], selh.t[0:4, h * 128:(h + 1) * 128], g_ws.t[0:4, 0:n], start=True, stop=True,
                  r=[selh.d, g_ws.d], w=[ps.d])
                I("dve", "scalar_tensor_tensor", ktT.t[:, h, 0:n], ps.t[:, 0:n], 128.0 ** -0.5, kmT.t[:, h, 0:n],
                  ALU.mult, ALU.mult, r=[ps.d, kmT.d], w=[ktT.d])
            ps = gen.next()
            for h in range(4):
                I("pe", "matmul", ps.t[:, 4 * h:4 * h + nch], selh.t[0:4, h * 128:(h + 1) * 128], g_s.t[0:4, 20:20 + nch],
                  start=True, stop=True, r=[selh.d, g_s.d], w=[ps.d])
            I("dve", "tensor_copy", decB.t[:, :], ps.t[:, 0:16], r=[ps.d], w=[decB.d])
            if full:
                ps = gen.next()
                for c in range(nch):
                    I("pe", "matmul", ps.t[:, 4 * c:4 * c + 4], g_fl.t[0:4, c * 128:(c + 1) * 128], id4.t[0:4, 0:4],
                      start=True, stop=True, r=[g_fl.d, id4.d], w=[ps.d])
                I("dve", "tensor_copy", floorT.t[:, 0:4 * nch], ps.t[:, 0:4 * nch], r=[ps.d], w=[floorT.d])

        def ln_post(L, nh, fl_ap, fl_dep, gm_ap, om_ap, om_dep, out_bf):
            st = stp.next()
            s = st.t
            I("dve", "scalar_tensor_tensor", s[0:L, 0:1], nh.t[0:L, 128:129], -1.0, nh.t[0:L, 128:129], ALU.mult, ALU.max,
              r=[nh.d], w=[st.d])
            I("dve", "tensor_tensor", s[0:L, 0:1], s[0:L, 0:1], fl_ap, ALU.max, r=[st.d, fl_dep], w=[st.d])
            I("dve", "reciprocal", s[0:L, 1:2], s[0:L, 0:1], r=[st.d], w=[st.d])
            hr = hrp.next()
            I("dve", "tensor_scalar", hr.t[0:L, :], nh.t[0:L, 0:128], s[0:L, 1:2], None, ALU.mult, r=[nh.d, st.d], w=[hr.d])
            sq = sqp.next()
            I("act", "activation", sq.t[0:L, :], hr.t[0:L, :], AF.Square, r=[hr.d], w=[sq.d])
            I("dve", "tensor_reduce", s[0:L, 2:3], hr.t[0:L, :], AX.X, ALU.add, r=[hr.d, st.d], w=[st.d])
            I("dve", "tensor_reduce", s[0:L, 3:4], sq.t[0:L, :], AX.X, ALU.add, r=[sq.d, st.d], w=[st.d])
            I("dve", "tensor_scalar", s[0:L, 4:5], s[0:L, 2:3], 1.0 / 128.0, None, ALU.mult, r=[st.d], w=[st.d])
            I("dve", "tensor_mul", s[0:L, 5:6], s[0:L, 4:5], s[0:L, 4:5], r=[st.d], w=[st.d])
            I("dve", "scalar_tensor_tensor", s[0:L, 6:7], s[0:L, 3:4], 1.0 / 128.0, s[0:L, 5:6], ALU.mult, ALU.subtract,
              r=[st.d], w=[st.d])
            I("dve", "tensor_scalar_max", s[0:L, 6:7], s[0:L, 6:7], 0.0, r=[st.d], w=[st.d])
            I("act", "activation", s[0:L, 7:8], s[0:L, 6:7], AF.Ln, bias=eps_t.t[0:L, 0:1], r=[st.d, eps_t.d], w=[st.d])
            I("act", "activation", s[0:L, 7:8], s[0:L, 7:8], AF.Exp, scale=-0.5, r=[st.d], w=[st.d])
            I("dve", "scalar_tensor_tensor", s[0:L, 8:9], s[0:L, 4:5], -1.0, s[0:L, 7:8], ALU.mult, ALU.mult,
              r=[st.d], w=[st.d])
            I("dve", "tensor_scalar", hr.t[0:L, :], hr.t[0:L, :], s[0:L, 7:8], s[0:L, 8:9], ALU.mult, ALU.add,
              r=[hr.d, st.d], w=[hr.d])
            I("dve", "tensor_mul", hr.t[0:L, :], hr.t[0:L, :], gm_ap, r=[hr.d, gmh.d], w=[hr.d])
            I("dve", "tensor_mul", out_bf.t[0:L, :], hr.t[0:L, :], om_ap, r=[hr.d, om_dep], w=[out_bf.d])

        def mlstm_units(n, full):
            units = []
            for c in range(n // 128):
                st_ = {}

                def partA(c=c, st_=st_):
                    cs = slice(c * 128, (c + 1) * 128)
                    kts_, nhs = [], []
                    for h in range(4):
                        I("dve", "tensor_scalar", Cst.t[:, h, :], Cst.t[:, h, :], decB.t[:, 4 * h + c:4 * h + c + 1], None,
                          ALU.mult, r=[dC[h], decB.d], w=[dC[h]])
                        pT = gen.next()
                        I("pe", "matmul", pT.t[:, 0:128], ktT.t[:, h, cs], ident.t[:], start=True, stop=True,
                          r=[ktT.d, ident.d], w=[pT.d])
                        kt = ktp.next()
                        I("act", "mul", kt.t[:], pT.t[:, 0:128], 1.0, r=[pT.d], w=[kt.d])
                        kts_.append(kt)
                        if full:
                            I("act", "mul", Cb.t[:, h, :], Cst.t[:, h, :], 1.0, r=[dC[h]], w=[dCb[h]])
                            pS = gen.next()
                            I("pe", "matmul", pS.t[:, 0:128], ktT.t[:, h, cs], qmT.t[:, h, cs], start=True, stop=True,
                              r=[ktT.d, qmT.d], w=[pS.d])
                            sd = sdp.next()
                            I("dve", "tensor_tensor", sd.t[:], pS.t[:, 0:128], tri.t[:], ALU.mult, r=[pS.d, tri.d], w=[sd.d])
                            pN = gen.next()
                            I("pe", "matmul", pN.t[:, 0:129], sd.t[:], vma.t[:, c, h, :], start=True, stop=False,
                              r=[sd.d, vma.d], w=[pN.d])
                            I("pe", "matmul", pN.t[:, 0:129], qmT.t[:, h, cs], Cb.t[:, h, :], start=False, stop=True,
                              r=[qmT.d, dCb[h]], w=[pN.d])
                            nh = nhp.next()
                            I("act", "mul", nh.t[:], pN.t[:, 0:129], 1.0, r=[pN.d], w=[nh.d])
                            nhs.append(nh)
                    for h in range(4):
                        pU = gen.next()
                        I("pe", "matmul", pU.t[:, 0:129], kts_[h].t[:], vma.t[:, c, h, :], start=True, stop=True,
                          r=[kts_[h].d, vma.d], w=[pU.d])
                        I("dve", "tensor_add", Cst.t[:, h, :], Cst.t[:, h, :], pU.t[:, 0:129], r=[dC[h], pU.d], w=[dC[h]])
                    st_["nhs"] = nhs

                def partB(c=c, st_=st_):
                    hbs = []
                    for h in range(4):
                        hb = hmp.next()
                        ln_post(128, st_["nhs"][h], floorT.t[:, 4 * c + h:4 * c + h + 1], floorT.d,
                                gmh.t[:, h * 128:(h + 1) * 128], omS.t[:, c, h * 128:(h + 1) * 128], omS.d, hb)
                        hbs.append(hb)
                    st_["hbs"] = hbs

                def partC(c=c, st_=st_):
                    cs = slice(c * 128, (c + 1) * 128)
                    for h in range(4):
                        pT2 = gen.next()
                        I("pe", "matmul", pT2.t[:, 0:128], st_["hbs"][h].t[:], ident.t[:], start=True, stop=True,
                          r=[st_["hbs"][h].d, ident.d], w=[pT2.d])
                        I("act", "mul", hmT.t[:, h, cs], pT2.t[:, 0:128], 1.0, r=[pT2.d], w=[hmT.d])
                units.append(partA)
                if full:
                    units.append(partB)
                    units.append(partC)
            return units

        def mlstm_group(n, full):
            for f_ in mlstm_units(n, full):
                f_()

        def conv_chunk(ch, n, out_bf_ap, out_dep):
            c_ = cv.next()
            I("dve", "tensor_scalar", c_.t[:, 0:n], u.t[:, ch, 0:n], cw.t[:, ch, 0:1], cb.t[:, ch:ch + 1], ALU.mult, ALU.add,
              r=[u.d, cw.d, cb.d], w=[c_.d])
            for j in range(1, 4):
                I("dve", "scalar_tensor_tensor", c_.t[:, 0:n], u.t[:, ch, j:j + n], cw.t[:, ch, j:j + 1], c_.t[:, 0:n],
                  ALU.mult, ALU.add, r=[u.d, cw.d, c_.d], w=[c_.d])
            I("act", "activation", out_bf_ap, c_.t[:, 0:n], AF.Silu, r=[c_.d], w=[out_dep])

        qaT = T("qaT", [128, 4, G], BF16)
        ptp = Pool("pT", 4, [128, G], BF16)
        pmp = Pool("pM", 3, [128, G], BF16)
        osb = Pool("osb", 2, [65, G], F32)
        rdn = Pool("rdn", 2, [128, G], F32)
        haT = T("haT", [64, 8, G], BF16)
        NKT = 16 + GT

        def attention_prompt(g, hook=None):
            kts = list(range(GT * g, GT * g + NKT))
            jobs = [(h, i_, kt) for h in range(8) for i_, kt in enumerate(kts)]
            pss = {}

            def emitS(j):
                h, i_, kt = jobs[j]
                hc, pb = h // 2, 64 * (h % 2)
                ps = psS.next()
                I("pe", "matmul", ps.t[:, 0:G], kaT.t[pb:pb + 64, hc, kt * 128:(kt + 1) * 128], qaT.t[pb:pb + 64, hc, :],
                  start=True, stop=True, r=[dka[kt // GT], qaT.d], w=[ps.d])
                pss[j] = ps
            DEPTH = 2
            for j in range(min(DEPTH, len(jobs))):
                emitS(j)
            po = None
            for j, (h, i_, kt) in enumerate(jobs):
                if i_ == 0:
                    po = psO.next()
                ps = pss.pop(j)
                pt = ptp.next()
                I("act", "activation", pt.t[:], ps.t[:, 0:G], AF.Exp, scale=0.125, r=[ps.d], w=[pt.d])
                pm = pmp.next()
                u0 = (2048 + G * g) - 128 * kt + 384
                I("dve", "tensor_tensor", pm.t[:], pt.t[:], cmask.t[:, u0:u0 + G], ALU.mult, r=[pt.d, cmask.d], w=[pm.d])
                I("pe", "matmul", po.t[0:65, 0:G], vaa.t[:, kt, h, :], pm.t[:], start=(i_ == 0), stop=(i_ == len(kts) - 1),
                  r=[dva[kt // GT], pm.d], w=[po.d])
                if j + DEPTH < len(jobs):
                    emitS(j + DEPTH)
                if i_ == len(kts) - 1:
                    ob = osb.next()
                    I("act", "mul", ob.t[:], po.t[0:65, 0:G], 1.0, r=[po.d], w=[ob.d])
                    pd = gen.next()
                    I("pe", "matmul", pd.t[0:64, 0:G], onesf.t[64:65, 0:64], ob.t[64:65, :], start=True, stop=True,
                      r=[onesf.d, ob.d], w=[pd.d])
                    rd = rdn.next()
                    I("dve", "reciprocal", rd.t[0:64, :], pd.t[0:64, 0:G], r=[pd.d], w=[rd.d])
                    I("dve", "tensor_mul", haT.t[:, h, :], ob.t[0:64, :], rd.t[0:64, :], r=[ob.d, rd.d], w=[haT.d])
                    if hook is not None:
                        hook(h)

        aT = T("aT", [128, 11, G], BF16)
        sgp = Pool("sg", 2, [128, G], F32)
        ysb = Pool("ysb", 2, [128, G], F32, dma=True)
        ones1 = T("ones1", [128, 128], BF16)
        I("pool", "memset", ones1.t[:], 1.0, w=[ones1.d])

        def resid_proj(nm, xt, n, srcs, att=False):
            for half in range(2):
                wt = load_w(nm, half * 512, 512)
                if att:
                    wa = wpool.next()
                    DMA("sp", wa.t[0:64, :, :], Wb["w_out"][512:1024, half * 512:(half + 1) * 512].rearrange("(h p) c -> p h c", p=64),
                        r=[dWb["w_out"]], w=[wa.d], sem=wa.d.sem)
                for cc in range(4):
                    ps = gen.next()
                    nk = len(srcs)
                    for i_, (kind, k, rhs, dep) in enumerate(srcs):
                        if kind == "w":
                            lhs, ld = wt.t[:, k, cc * 128:(cc + 1) * 128], wt.d
                        else:
                            lhs, ld = wa.t[0:64, k, cc * 128:(cc + 1) * 128], wa.d
                        I("pe", "matmul", ps.t[:, 0:n], lhs, rhs, start=(i_ == 0), stop=(i_ == nk - 1), r=[ld, dep], w=[ps.d])
                    ch = half * 4 + cc
                    I("dve", "tensor_add", xt.t[:, ch, 0:n], xt.t[:, ch, 0:n], ps.t[:, 0:n], r=[xt.d, ps.d], w=[xt.d])

        def xattn(xt, n, mk_ap_fn, mv_ap_fn, mdeps, per_seq, pre=None):
            h1 = hpool.next()
            norm(xt, n, 2, h1)
            qx = hpool.next()
            for half in range(2):
                def ev_q(cc, ps, m, half=half):
                    I("act", "mul", qx.t[:, half * 4 + cc, 0:n], ps.t[:, 0:n], 1.0, r=[ps.d], w=[qx.d])
                proj_fm("w_mq", half * 512, 512, h1, n, ev_q)
            oT = hpool.next()
            groups = [(0, n)] if not per_seq else [(4 * s_, 4) for s_ in range(16)]
            jobs = [(gi, t0, tn, h) for gi, (t0, tn) in enumerate(groups) for h in range(4)]

            def emit_scores(job):
                gi, t0, tn, h = job
                if pre is not None and h == 0:
                    pre(gi)
                pts = []
                for mb in range(2):
                    ps = psS.next()
                    for ec in range(2):
                        I("pe", "matmul", ps.t[:, 0:tn], mk_ap_fn(gi, h, ec, mb), qx.t[:, 2 * h + ec, t0:t0 + tn],
                          start=(ec == 0), stop=(ec == 1), r=mdeps(gi) + [qx.d], w=[ps.d])
                    pt = ptp.next()
                    I("act", "activation", pt.t[:, 0:tn], ps.t[:, 0:tn], AF.Exp, scale=1.0 / 16.0, r=[ps.d], w=[pt.d])
                    pts.append(pt)
                return pts
            pend = emit_scores(jobs[0])
            for ji, (gi, t0, tn, h) in enumerate(jobs):
                pts = pend
                if ji + 1 < len(jobs):
                    pend = emit_scores(jobs[ji + 1])
                pd = gen.next()
                for mb in range(2):
                    I("pe", "matmul", pd.t[:, 0:tn], ones1.t[:], pts[mb].t[:, 0:tn], start=(mb == 0), stop=(mb == 1),
                      r=[ones1.d, pts[mb].d], w=[pd.d])
                rd = rdn.next()
                I("dve", "reciprocal", rd.t[:, 0:tn], pd.t[:, 0:tn], r=[pd.d], w=[rd.d])
                for ec in range(2):
                    po = gen.next()
                    for mb in range(2):
                        I("pe", "matmul", po.t[:, 0:tn], mv_ap_fn(gi, h, ec, mb), pts[mb].t[:, 0:tn],
                          start=(mb == 0), stop=(mb == 1), r=mdeps(gi) + [pts[mb].d], w=[po.d])
                    I("dve", "tensor_mul", oT.t[:, 2 * h + ec, t0:t0 + tn], po.t[:, 0:tn], rd.t[:, 0:tn],
                      r=[po.d, rd.d], w=[oT.d])
            resid_proj("w_mo", xt, n, [("w", k, oT.t[:, k, 0:n], oT.d) for k in range(8)])

        def ffn(xt, n):
            h2 = hpool.next()
            norm(xt, n, 3, h2)
            for hf in range(2):
                base = hf * 1408
                for c0 in (0, 512, 1024):
                    ncols = min(512, 1408 - c0)
                    wg = load_w("w_gate", base + c0, ncols)
                    wu = load_w("w_up", base + c0, ncols)
                    for cc in range(ncols // 128):
                        f = c0 // 128 + cc
                        pg = gen.next()
                        pu = gen.next()
                        for k in range(8):
                            I("pe", "matmul", pg.t[:, 0:n], wg.t[:, k, cc * 128:(cc + 1) * 128], h2.t[:, k, 0:n],
                              start=(k == 0), stop=(k == 7), r=[wg.d, h2.d], w=[pg.d])
                        for k in range(8):
                            I("pe", "matmul", pu.t[:, 0:n], wu.t[:, k, cc * 128:(cc + 1) * 128], h2.t[:, k, 0:n],
                              start=(k == 0), stop=(k == 7), r=[wu.d, h2.d], w=[pu.d])
                        sg = sgp.next()
                        I("act", "activation", sg.t[:, 0:n], pg.t[:, 0:n], AF.Silu, r=[pg.d], w=[sg.d])
                        I("dve", "tensor_mul", aT.t[:, f, 0:n], sg.t[:, 0:n], pu.t[:, 0:n], r=[sg.d, pu.d], w=[aT.d])
                for ch in range(8):
                    wd = wdpool.next()
                    DMA("sp", wd.t[:], Wb["w_down"][base:base + 1408, ch * 128:(ch + 1) * 128].rearrange("(k p) c -> p k c", p=128),
                        r=[dWb["w_down"]], w=[wd.d], sem=wd.d.sem)
                    ps = gen.next()
                    for k in range(11):
                        I("pe", "matmul", ps.t[:, 0:n], wd.t[:, k, :], aT.t[:, k, 0:n],
                          start=(k == 0), stop=(k == 10), r=[wd.d, aT.d], w=[ps.d])
                    I("dve", "tensor_add", xt.t[:, ch, 0:n], xt.t[:, ch, 0:n], ps.t[:, 0:n], r=[xt.d, ps.d], w=[xt.d])

        def final_out(xt, n, out_fn):
            rs = norm(xt, n, 4, None)
            for k in range(8):
                y = ysb.next()
                I("dve", "scalar_tensor_tensor", y.t[:, 0:n], xt.t[:, k, 0:n], gv.t[:, 4, k:k + 1], rs.t[:, 0:n],
                  ALU.mult, ALU.mult, r=[xt.d, gv.d, rs.d], w=[y.d])
                DMA("sp", out_fn(k), y.t[:, 0:n], r=[y.d], sem=y.d.sem)

        KD = _os2.environ.get("KDUMP", "")

        def dump_rows(src_ap, dep, row0, nrows, g):
            y = ysb.next()
            I("dve", "tensor_copy", y.t[0:nrows, 0:G], src_ap, r=[dep], w=[y.d])
            DMA("sp", o_yT[row0:row0 + nrows, g * G:(g + 1) * G], y.t[0:nrows, 0:G], r=[y.d], sem=y.d.sem)

        precast("w_in", only=(512, 1024, 2048))
        _late = ["w_in_rest", "mem", "w_out", "w_mq", "w_mo", "w_gate", "w_up", "w_down"]
        for blk in range(int(_os2.environ.get('KBLK0', '0')), 4):
            local = blk == 3
            halo = blk == 2
            for g in range(NG):
                xt = xpool.next()
                DMA("sp", xt.t[:], xblk[blk, :, g * G:(g + 1) * G].rearrange("(k p) n -> p k n", p=128),
                    w=[xt.d], sem=xt.d.sem)
                hT = hpool.next()
                norm(xt, G, 0, hT)
                ckpt('g_norm')
                need_q = local or (halo and g == NG - 1)
                if need_q:
                    def ev_u(cc, ps, m):
                        I("act", "mul", u.t[:, cc, 3:3 + G], ps.t[:, 0:G], 1.0, r=[ps.d], w=[u.d])
                    proj_fm("w_in", WIN["qm"], 512, hT, G, ev_u)

                def ev_uk(cc, ps, m):
                    I("act", "mul", u.t[:, 4 + cc, 3:3 + G], ps.t[:, 0:G], 1.0, r=[ps.d], w=[u.d])
                proj_fm("w_in", WIN["km"], 512, hT, G, ev_uk)
                for ch in range(8):
                    if ch < 4 and not local:
                        continue
                    if ch < 4:
                        conv_chunk(ch, G, qmT.t[:, ch, :], qmT.d)
                    else:
                        conv_chunk(ch, G, kmT.t[:, ch - 4, :], kmT.d)
                if local and g == NG - 1:
                    DMA("sp", o_convT.rearrange("(k p) j -> p k j", p=128), u.t[:, :, G:G + 3], r=[u.d], sem=u.d.sem,
                        allow_slow_non_contiguous=True)
                I("dve", "tensor_copy", u.t[:, :, 0:3], u.t[:, :, G:G + 3], r=[u.d], w=[u.d])
                ckpt('g_conv')

                def ev_vm(tt, ps, m):
                    I("act", "mul", vma.t[:, tt, :, 0:128], ps.t[:, 0:512].rearrange("p (h e) -> p h e", e=128), 1.0,
                      r=[ps.d], w=[vma.d])
                proj_tm("w_in", WIN["vm"], 512, hT, G, ev_vm)
                if local:
                    def ev_om(tt, ps, m):
                        I("act", "activation", omS.t[:, tt, :], ps.t[:, 0:512], AF.Sigmoid, r=[ps.d], w=[omS.d])
                    proj_tm("w_in", WIN["om"], 512, hT, G, ev_om)
                wt = load_w("w_in", WIN["ig"], 8)
                for gi_ in range(2):
                    ps = gen.next()
                    for k in range(8):
                        I("pe", "matmul", ps.t[0:4, 0:G], wt.t[:, k, 4 * gi_:4 * gi_ + 4], hT.t[:, k, :],
                          start=(k == 0), stop=(k == 7), r=[wt.d, hT.d], w=[ps.d])
                    I("act", "mul", gz.t[:, gi_, :], ps.t[0:4, 0:G], 1.0, r=[ps.d], w=[gz.d])
                gates(G, blk, local)
                ckpt('g_gates')
                if local:
                    _units = mlstm_units(G, True)
                else:
                    mlstm_group(G, False)
                ckpt('g_mlstm')
                if _late:
                    nm_ = _late.pop(0)
                    if nm_ == "w_in_rest":
                        precast("w_in", skip=(512, 1024, 2048))
                    elif nm_ == "mem":
                        mem_phase()
                    else:
                        precast(nm_)
                if halo or local:
                    kg = (0 if halo else NG) + g

                    def ev_ka(cc, ps, m, kg=kg):
                        if local:
                            st = kstg.next()
                            I("dve", "tensor_copy", st.t[:, 0:G], ps.t[:, 0:G], r=[ps.d], w=[st.d])
                            I("act", "mul", kaT.t[:, cc, kg * G:(kg + 1) * G], st.t[:, 0:G], 1.0,
                              r=[st.d], w=[dka[kg]])
                            DMA("sp", o_kaT[cc * 128:(cc + 1) * 128, g * G:(g + 1) * G], st.t[:, 0:G], r=[st.d], sem=st.d.sem)
                        else:
                            I("act", "mul", kaT.t[:, cc, kg * G:(kg + 1) * G], ps.t[:, 0:G], 1.0,
                              r=[ps.d], w=[dka[kg]])
                    proj_fm("w_in", WIN["ka"], 512, hT, G, ev_ka)

                    def ev_va(tt, ps, m, kg=kg):
                        if local:
                            st = kstg.next()
                            I("dve", "tensor_copy", st.t[:], ps.t[:, 0:512], r=[ps.d], w=[st.d])
                            I("act", "mul", vaa.t[:, kg * GT + tt, :, 0:64],
                              st.t[:, 0:512].rearrange("p (h e) -> p h e", e=64), 1.0, r=[st.d], w=[dva[kg]])
                            DMA("sp", o_va[g * G + tt * 128:g * G + (tt + 1) * 128, :], st.t[:], r=[st.d], sem=st.d.sem)
                        else:
                            I("dve", "tensor_scalar", vaa.t[:, kg * GT + tt, :, 0:64],
                              ps.t[:, 0:512].rearrange("p (h e) -> p h e", e=64), hfl.t[:, 0:1], None, ALU.mult,
                              r=[ps.d, hfl.d], w=[dva[kg]])
                    proj_tm("w_in", WIN["va"], 512, hT, G, ev_va)
                    ckpt('g_kv')
                if local:
                    def ev_qa(cc, ps, m):
                        I("act", "mul", qaT.t[:, cc, :], ps.t[:, 0:G], 1.0, r=[ps.d], w=[qaT.d])
                    proj_fm("w_in", WIN["qa"], 512, hT, G, ev_qa)
                    def _hook(h_, _units=_units):
                        if _units:
                            _units.pop(0)()
                    attention_prompt(g, _hook)
                    while _units:
                        _units.pop(0)()
                    ckpt('g_attn')
                    if KD == "qk":
                        for k in range(4):
                            dump_rows(qmT.t[:, k, :], qmT.d, k * 128, 128, g)
                            dump_rows(kmT.t[:, k, :], kmT.d, 512 + k * 128, 128, g)
                        continue
                    if KD == "h":
                        for k in range(8):
                            dump_rows(hT.t[:, k, :], hT.d, k * 128, 128, g)
                        continue
                    if KD == "mix":
                        for k in range(4):
                            dump_rows(hmT.t[:, k, :], hmT.d, k * 128, 128, g)
                        for h_ in range(8):
                            dump_rows(haT.t[:, h_, :], haT.d, 512 + h_ * 64, 64, g)
                        continue
                    srcs = [("w", k, hmT.t[:, k, :], hmT.d) for k in range(4)] + \
                           [("a", h, haT.t[:, h, :], haT.d) for h in range(8)]
                    resid_proj("w_out", xt, G, srcs, att=True)
                    ckpt('g_wout')
                    if KD == "x1":
                        for k in range(8):
                            dump_rows(xt.t[:, k, :], xt.d, k * 128, 128, g)
                        continue
                    xattn(xt, G,
                          lambda gi, h, ec, mb: mkT.t[:, 2 * h + ec, mb * 128:(mb + 1) * 128],
                          lambda gi, h, ec, mb: mv.t[:, mb, (2 * h + ec) * 128:(2 * h + ec + 1) * 128],
                          lambda gi: [mkT.d, mv.d], False)
                    ckpt('g_xattn')
                    if KD == "x2":
                        for k in range(8):
                            dump_rows(xt.t[:, k, :], xt.d, k * 128, 128, g)
                        continue
                    ffn(xt, G)
                    ckpt('g_ffn')
                    if KD == "x3":
                        for k in range(8):
                            dump_rows(xt.t[:, k, :], xt.d, k * 128, 128, g)
                        continue
                    final_out(xt, G, lambda k, g=g: o_yT[k * 128:(k + 1) * 128, g * G:(g + 1) * G])
        DMA("sp", o_C.rearrange("h d e -> d h e"), Cst.t[:], r=[Cst.d] + dC, sem=Cst.d.sem)
        mo = T("mo", [4, 1], F32, dma=True)
        I("dve", "tensor_add", mo.t[:], Bc.t[:], Mc.t[:], r=[Bc.d, Mc.d], w=[mo.d])
        DMA("sp", o_m, mo.t[:], r=[mo.d], sem=mo.d.sem)


        ckpt("prompt")
        NS = 64

        def carve(ap, name, parents, dma=True):
            t_ = type("CT", (), {})()
            t_.t = ap
            t_.d = P.dep(name, dma=dma)
            rd_ = []
            for pd_ in parents:
                rd_.extend(pd_.readers)
                if pd_.writer is not None:
                    rd_.append(pd_.writer)
            t_.d.readers = rd_
            return t_

        kpar = dka + [kaT.d]
        vpar = dva + [vaa.d]
        kf = kaT.t[:].rearrange("p a b -> p (a b)")
        vfl = vaa.t[:].rearrange("p a b c -> p (a b c)")
        uf = u.t[:].rearrange("p a b -> p (a b)")
        cf = cmask.t[:]
        sK = [carve(kf[:, i * 3600:(i + 1) * 3600].rearrange("p (c n) -> p c n", n=900), "sK%d" % i, kpar) for i in range(2)]
        smk = [carve(kf[:, 7200 + i * 2048:7200 + (i + 1) * 2048].rearrange("p (c n) -> p c n", n=256), "smk%d" % i, kpar)
               for i in range(2)]
        smv = [carve(kf[:, 11296 + i * 2048:11296 + (i + 1) * 2048].rearrange("p (c n) -> p c n", n=1024), "smv%d" % i, kpar)
               for i in range(2)]
        sV = [carve(vfl[:, i * 3584:(i + 1) * 3584].rearrange("p (c n) -> p c n", n=512), "sV%d" % i, vpar) for i in range(2)]
        us = carve(uf[:, 0:896].rearrange("p (k s j) -> p k s j", k=8, s=16), "us", [u.d])
        Cq = [carve(uf[:, 896 + i * 516:896 + (i + 1) * 516].rearrange("p (h e) -> p h e", e=129), "Cq%d" % i, [u.d])
              for i in range(2)]
        cpar = [cmask.d]
        bd = carve(cf[0:32, 0:512], "bd", cpar)
        vaS = carve(cf[0:64, 512:1024], "vaS", cpar)
        sVn = carve(cf[0:4, 1024:1536], "sVn", cpar)
        kaS = carve(cf[:, 1536:1792].rearrange("p (c n) -> p c n", n=64), "kaS", cpar)
        Qbd = carve(cf[:, 1792:1824].rearrange("p (c n) -> p c n", n=8), "Qbd", cpar)
        qmk = [carve(cf[:, 1824 + i * 64:1888 + i * 64], "qmk%d" % i, cpar) for i in range(2)]
        ktm = [carve(cf[0:64, 1952 + i * 128:2080 + i * 128], "ktm%d" % i, cpar) for i in range(2)]
        ktall = carve(cf[0:64, 2224:2736].rearrange("p (h d) -> p h d", d=128), "ktall", cpar)
        rowm = T("rowm", [64, 16], F32, dma=True)
        m0s = T("m0s", [4, 32], F32, dma=True)
        DMA("sp", rowm.t[:], cin["c_rowm"], w=[rowm.d], sem=rowm.d.sem)
        DMA("sp", m0s.t[:, 0:16], s_m0T, w=[m0s.d], sem=m0s.d.sem)
        DMA("pool", bd.t, cin["c_bd"], w=[bd.d], sem=bd.d.sem)

        xt = xpool.next()
        DMA("sp", xt.t[:, :, 0:NS], xsT.rearrange("(k p) n -> p k n", p=128), w=[xt.d], sem=xt.d.sem)
        hT = hpool.next()
        norm(xt, NS, 0, hT)
        for k in range(8):
            DMA("sp", us.t[:, k, :, 0:3], s_convT[k * 128:(k + 1) * 128, :, :], w=[us.d], sem=us.d.sem)

        def ev_us(off):
            def f_(cc, ps, m):
                I("act", "mul", us.t[:, off + cc, :, 3:7], ps.t[:, 0:NS].rearrange("p (s t) -> p s t", t=4), 1.0,
                  r=[ps.d], w=[us.d])
            return f_
        proj_fm("w_in", WIN["qm"], 512, hT, NS, ev_us(0))
        proj_fm("w_in", WIN["km"], 512, hT, NS, ev_us(4))
        for ch in range(8):
            c_ = cv.next()
            cvw = c_.t[:, 0:NS].rearrange("p (s t) -> p s t", t=4)
            I("dve", "tensor_scalar", cvw, us.t[:, ch, :, 0:4], cw.t[:, ch, 0:1], cb.t[:, ch:ch + 1], ALU.mult, ALU.add,
              r=[us.d, cw.d, cb.d], w=[c_.d])
            for j in range(1, 4):
                I("dve", "scalar_tensor_tensor", cvw, us.t[:, ch, :, j:j + 4], cw.t[:, ch, j:j + 1], cvw, ALU.mult, ALU.add,
                  r=[us.d, cw.d, c_.d], w=[c_.d])
            dst = qmT if ch < 4 else kmT
            I("act", "activation", dst.t[:, ch % 4, 0:NS], c_.t[:, 0:NS], AF.Silu, r=[c_.d], w=[dst.d])
        for k in range(8):
            DMA("sp", o_sconvT[k * 128:(k + 1) * 128, :, :], us.t[:, k, :, 4:7], r=[us.d], sem=us.d.sem)

        def ev_vms(tt, ps, m):
            I("act", "mul", vma.t[0:NS, 0, :, 0:128], ps.t[0:NS, 0:512].rearrange("p (h e) -> p h e", e=128), 1.0,
              r=[ps.d], w=[vma.d])
        proj_tm("w_in", WIN["vm"], 512, hT, NS, ev_vms)

        def ev_oms(tt, ps, m):
            I("act", "activation", omS.t[0:NS, 0, :], ps.t[0:NS, 0:512], AF.Sigmoid, r=[ps.d], w=[omS.d])
        proj_tm("w_in", WIN["om"], 512, hT, NS, ev_oms)
        wt = load_w("w_in", WIN["ig"], 8)
        for gi_ in range(2):
            ps = gen.next()
            for k in range(8):
                I("pe", "matmul", ps.t[0:4, 0:NS], wt.t[:, k, 4 * gi_:4 * gi_ + 4], hT.t[:, k, 0:NS],
                  start=(k == 0), stop=(k == 7), r=[wt.d, hT.d], w=[ps.d])
            I("act", "mul", gz.t[:, gi_, 0:NS], ps.t[0:4, 0:NS], 1.0, r=[ps.d], w=[gz.d])
        v3 = lambda ap: ap.rearrange("p (s t) -> p s t", t=4)
        zi = gz.t[0:4, 0, 0:NS]
        zf = gz.t[0:4, 1, 0:NS]
        I("act", "activation", g_e.t[:, 0:NS], zf, AF.Exp, bias=nbf.t[:, 0:1], scale=-1.0, r=[gz.d, nbf.d], w=[g_e.d])
        I("act", "activation", g_e.t[:, 0:NS], g_e.t[:, 0:NS], AF.Ln, bias=one4.t[:, 0:1], r=[g_e.d, one4.d], w=[g_e.d])
        I("dve", "tensor_scalar", g_ws.t[:, 0:NS], g_e.t[:, 0:NS], -1.0, None, ALU.mult, r=[g_e.d], w=[g_ws.d])
        Bv, lv, Gv = v3(g_B.t[:, 0:NS]), v3(g_ws.t[:, 0:NS]), v3(g_G.t[:, 0:NS])
        I("dve", "tensor_copy", Bv[:, :, 0:1], lv[:, :, 0:1], r=[g_ws.d], w=[g_B.d])
        for t in range(1, 4):
            I("dve", "tensor_add", Bv[:, :, t:t + 1], Bv[:, :, t - 1:t], lv[:, :, t:t + 1], r=[g_ws.d, g_B.d], w=[g_B.d])
        I("dve", "tensor_scalar", g_G.t[:, 0:NS], zi, bg.t[:, 0:1], None, ALU.add, r=[gz.d, bg.d], w=[g_G.d])
        I("dve", "tensor_sub", g_G.t[:, 0:NS], g_G.t[:, 0:NS], g_B.t[:, 0:NS], r=[g_G.d, g_B.d], w=[g_G.d])
        Ms = g_e.t[:, 64:80]
        I("dve", "tensor_reduce", Ms, Gv, AX.X, ALU.max, r=[g_G.d, g_e.d], w=[g_e.d])
        I("dve", "tensor_tensor", Ms, Ms, m0s.t[:, 0:16], ALU.max, r=[g_e.d, m0s.d], w=[g_e.d])
        wv, fv = v3(g_ws.t[:, 0:NS]), v3(g_fl.t[:, 0:NS])
        for t in range(4):
            I("dve", "tensor_sub", wv[:, :, t], Gv[:, :, t], Ms, r=[g_G.d, g_e.d, g_ws.d], w=[g_ws.d])
            I("dve", "scalar_tensor_tensor", fv[:, :, t], Bv[:, :, t], -1.0, Ms, ALU.mult, ALU.subtract,
              r=[g_B.d, g_e.d, g_fl.d], w=[g_fl.d])
        I("act", "activation", g_ws.t[:, 0:NS], g_ws.t[:, 0:NS], AF.Exp, r=[g_ws.d], w=[g_ws.d])
        I("act", "activation", g_fl.t[:, 0:NS], g_fl.t[:, 0:NS], AF.Exp, r=[g_fl.d], w=[g_fl.d])
        I("dve", "tensor_sub", g_e.t[:, 80:96], m0s.t[:, 0:16], Ms, r=[m0s.d, g_e.d], w=[g_e.d])
        I("act", "activation", g_e.t[:, 80:96], g_e.t[:, 80:96], AF.Exp, r=[g_e.d], w=[g_e.d])
        I("dve", "tensor_add", m0s.t[:, 16:32], Bv[:, :, 3], Ms, r=[g_B.d, g_e.d, m0s.d], w=[m0s.d])
        DMA("sp", o_sm, m0s.t[:, 16:32], r=[m0s.d], sem=m0s.d.sem)
        for h in range(4):
            ps = gen.next()
            I("pe", "matmul", ps.t[:, 0:NS], selh.t[0:4, h * 128:(h + 1) * 128], g_ws.t[0:4, 0:NS], start=True, stop=True,
              r=[selh.d, g_ws.d], w=[ps.d])
            I("dve", "scalar_tensor_tensor", ktT.t[:, h, 0:NS], ps.t[:, 0:NS], 128.0 ** -0.5, kmT.t[:, h, 0:NS],
              ALU.mult, ALU.mult, r=[ps.d, kmT.d], w=[ktT.d])
        decS = rspool.next()
        ps = gen.next()
        for h in range(4):
            I("pe", "matmul", ps.t[:, 16 * h:16 * h + 16], selh.t[0:4, h * 128:(h + 1) * 128], g_e.t[0:4, 80:96],
              start=True, stop=True, r=[selh.d, g_e.d], w=[ps.d])
        I("dve", "tensor_copy", decS.t[:, 0:64], ps.t[:, 0:64], r=[ps.d], w=[decS.d])
        ps = gen.next()
        I("pe", "matmul", ps.t[0:NS, 0:4], g_fl.t[0:4, 0:NS], id4.t[0:4, 0:4], start=True, stop=True,
          r=[g_fl.d, id4.d], w=[ps.d])
        I("dve", "tensor_copy", floorT.t[0:NS, 0:4], ps.t[0:NS, 0:4], r=[ps.d], w=[floorT.d])
        pNs = []
        for h in range(4):
            pT = psS.next()
            I("pe", "matmul", pT.t[0:NS, 0:128], ktT.t[:, h, 0:NS], ident.t[:], start=True, stop=True,
              r=[ktT.d, ident.d], w=[pT.d])
            I("act", "mul", ktall.t[:, h, :], pT.t[0:NS, 0:128], 1.0, r=[pT.d], w=[ktall.d])
            pS_ = psO.next()
            I("pe", "matmul", pS_.t[0:NS, 0:NS], ktT.t[:, h, 0:NS], qmT.t[:, h, 0:NS], start=True, stop=True,
              r=[ktT.d, qmT.d], w=[pS_.d])
            sd = sdp.next()
            I("dve", "tensor_tensor", sd.t[0:NS, 0:NS], pS_.t[0:NS, 0:NS], tri64.t[:, :], ALU.mult,
              r=[pS_.d, tri64.d], w=[sd.d])
            pN = gen.next()
            I("pe", "matmul", pN.t[0:NS, 0:129], sd.t[0:NS, 0:NS], vma.t[0:NS, 0, h, :], start=True, stop=False,
              r=[sd.d, vma.d], w=[pN.d])
            pNs.append(pN)
        for s_ in range(16):
            cq = Cq[s_ % 2]
            DMA("sp", cq.t[:, :, 0:128], s_C0[s_].rearrange("h d e -> d h e"), w=[cq.d], sem=cq.d.sem)
            DMA("sp", cq.t[:, :, 128], s_n0T[:, s_, :], w=[cq.d], sem=cq.d.sem, allow_slow_non_contiguous=True)
            dq = [P.dep("cq%d_%d" % (s_, h_)) for h_ in range(4)]
            for d_ in dq:
                d_.writer = cq.d.writer
            qks, kms = [], []
            for h in range(4):
                I("dve", "tensor_scalar", cq.t[:, h, :], cq.t[:, h, :], decS.t[:, 16 * h + s_:16 * h + s_ + 1], None, ALU.mult,
                  r=[cq.d, decS.d], w=[dq[h]])
                I("act", "mul", Cb.t[:, h, :], cq.t[:, h, :], 1.0, r=[dq[h]], w=[dCb[h]])
                qk_ = hmp.next()
                I("dve", "tensor_tensor", qk_.t[:, 0:NS], qmT.t[:, h, 0:NS], seqmask.t[:, s_, :], ALU.mult,
                  r=[qmT.d, seqmask.d], w=[qk_.d])
                km_ = ktp.next()
                I("dve", "tensor_scalar", km_.t[0:NS, :], ktall.t[:, h, :], rowm.t[:, s_:s_ + 1], None, ALU.mult,
                  r=[ktall.d, rowm.d], w=[km_.d])
                qks.append(qk_)
                kms.append(km_)
            pUs = []
            for h in range(4):
                I("pe", "matmul", pNs[h].t[0:NS, 0:129], qks[h].t[:, 0:NS], Cb.t[:, h, :], start=False, stop=(s_ == 15),
                  r=[qks[h].d, dCb[h]], w=[pNs[h].d])
                pU = (psS if h % 2 == 0 else psO).next()
                I("pe", "matmul", pU.t[:, 0:129], kms[h].t[0:NS, :], vma.t[0:NS, 0, h, :], start=True, stop=True,
                  r=[kms[h].d, vma.d], w=[pU.d])
                pUs.append(pU)
            for h in range(4):
                I("dve", "tensor_add", cq.t[:, h, :], cq.t[:, h, :], pUs[h].t[:, 0:129], r=[dq[h], pUs[h].d], w=[dq[h]])
            DMA("sp", o_sC[s_].rearrange("h d e -> d h e"), cq.t[:, :, :], r=dq + [cq.d], w=[cq.d], sem=cq.d.sem)
        for h in range(4):
            nh = nhp.next()
            I("act", "mul", nh.t[0:NS, :], pNs[h].t[0:NS, 0:129], 1.0, r=[pNs[h].d], w=[nh.d])
            hb = hmp.next()
            ln_post(NS, nh, floorT.t[0:NS, h:h + 1], floorT.d, gmh.t[0:NS, h * 128:(h + 1) * 128],
                    omS.t[0:NS, 0, h * 128:(h + 1) * 128], omS.d, hb)
            pT = psS.next()
            I("pe", "matmul", pT.t[:, 0:NS], hb.t[0:NS, :], ident.t[0:NS, 0:NS], start=True, stop=True,
              r=[hb.d, ident.d], w=[pT.d])
            I("act", "mul", hmT.t[:, h, 0:NS], pT.t[:, 0:NS], 1.0, r=[pT.d], w=[hmT.d])
        ckpt("s_mlstm")
        def ev_kas(cc, ps, m):
            st = kstg.next()
            I("dve", "tensor_copy", st.t[:, 0:NS], ps.t[:, 0:NS], r=[ps.d], w=[st.d])
            I("act", "mul", kaS.t[:, cc, :], st.t[:, 0:NS], 1.0, r=[st.d], w=[kaS.d])
            DMA("sp", o_skaT[cc * 128:(cc + 1) * 128, :], st.t[:, 0:NS], r=[st.d], sem=st.d.sem)
        proj_fm("w_in", WIN["ka"], 512, hT, NS, ev_kas)

        def ev_vas(tt, ps, m):
            st = kstg.next()
            I("dve", "tensor_copy", st.t[0:NS, :], ps.t[0:NS, 0:512], r=[ps.d], w=[st.d])
            I("act", "mul", vaS.t, st.t[0:NS, :], 1.0, r=[st.d], w=[vaS.d])
            DMA("sp", o_sva, st.t[0:NS, :], r=[st.d], sem=st.d.sem)
        proj_tm("w_in", WIN["va"], 512, hT, NS, ev_vas)

        def ev_qas(cc, ps, m):
            I("act", "mul", qaT.t[:, cc, 0:NS], ps.t[:, 0:NS], 1.0, r=[ps.d], w=[qaT.d])
        proj_fm("w_in", WIN["qa"], 512, hT, NS, ev_qas)
        I("pool", "memset", Qbd.t, 0.0, w=[Qbd.d])
        smf = smask.t[:].rearrange("p a b -> p (a b)")
        for s_ in range(16):
            k_, v_ = sK[s_ % 2], sV[s_ % 2]
            DMA("pool", k_.t[:, :, 0:896], s_kT[s_].rearrange("(c p) n -> p c n", p=128), w=[k_.d], sem=k_.d.sem)
            I("act", "mul", k_.t[:, :, 896:900], kaS.t[:, :, 4 * s_:4 * s_ + 4], 1.0, r=[kaS.d, k_.d], w=[k_.d])
            DMA("pool", v_.t, s_v[s_].rearrange("(t p) c -> p t c", p=128), w=[v_.d], sem=v_.d.sem)
            DMA("sp", sVn.t, vaS.t[4 * s_:4 * s_ + 4, :], r=[vaS.d], w=[sVn.d], sem=sVn.d.sem)
            I("act", "mul", Qbd.t[0:64, :, 0:4], qaT.t[0:64, :, 4 * s_:4 * s_ + 4], 1.0, r=[qaT.d, Qbd.d], w=[Qbd.d])
            I("act", "mul", Qbd.t[64:128, :, 4:8], qaT.t[64:128, :, 4 * s_:4 * s_ + 4], 1.0, r=[qaT.d, Qbd.d], w=[Qbd.d])
            ps = psS.next()
            for kt in range(8):
                for c in range(4):
                    if kt < 7:
                        I("pe", "matmul", ps.t[:, kt * 32 + 8 * c:kt * 32 + 8 * c + 8], k_.t[:, c, kt * 128:(kt + 1) * 128],
                          Qbd.t[:, c, :], start=True, stop=True, r=[k_.d, Qbd.d], w=[ps.d])
                    else:
                        I("pe", "matmul", ps.t[0:4, 224 + 8 * c:232 + 8 * c], k_.t[:, c, 896:900],
                          Qbd.t[:, c, :], start=True, stop=True, r=[k_.d, Qbd.d], w=[ps.d])
            pt = ptp.next()
            I("act", "activation", pt.t[:, 0:224], ps.t[:, 0:224], AF.Exp, scale=0.125, r=[ps.d], w=[pt.d])
            I("act", "activation", pt.t[0:4, 224:256], ps.t[0:4, 224:256], AF.Exp, scale=0.125, r=[ps.d, pt.d], w=[pt.d])
            pm = pmp.next()
            I("dve", "tensor_tensor", pm.t[:, 0:224], pt.t[:, 0:224], smf[:, 0:224], ALU.mult, r=[pt.d, smask.d], w=[pm.d])
            I("dve", "tensor_tensor", pm.t[0:4, 224:256], pt.t[0:4, 224:256], smf[0:4, 224:256], ALU.mult,
              r=[pt.d, smask.d, pm.d], w=[pm.d])
            po = psO.next()
            pd = gen.next()
            for kt in range(8):
                K_ = 128 if kt < 7 else 4
                rhs = v_.t[:, kt, :] if kt < 7 else sVn.t
                rdep = v_.d if kt < 7 else sVn.d
                I("pe", "matmul", po.t[0:32, 0:512], pm.t[0:K_, kt * 32:(kt + 1) * 32], rhs, start=(kt == 0), stop=(kt == 7),
                  r=[pm.d, rdep], w=[po.d])
            for kt in range(8):
                K_ = 128 if kt < 7 else 4
                I("pe", "matmul", pd.t[0:32, 0:1], pm.t[0:K_, kt * 32:(kt + 1) * 32], ones1.t[0:K_, 0:1],
                  start=(kt == 0), stop=(kt == 7), r=[pm.d, ones1.d], w=[pd.d])
            ot = kstg.next()
            I("dve", "tensor_tensor", ot.t[0:32, :], po.t[0:32, 0:512], bd.t, ALU.mult, r=[po.d, bd.d], w=[ot.d])
            od = hrp.next()
            I("dve", "tensor_reduce", od.t[0:32, 0:64], ot.t[0:32, :].rearrange("p (h e) -> p e h", e=64), AX.X, ALU.add,
              r=[ot.d], w=[od.d])
            stq = stp.next()
            I("dve", "reciprocal", stq.t[0:32, 0:1], pd.t[0:32, 0:1], r=[pd.d], w=[stq.d])
            odn = hmp.next()
            I("dve", "tensor_scalar", odn.t[0:32, 0:64], od.t[0:32, 0:64], stq.t[0:32, 0:1], None, ALU.mult,
              r=[od.d, stq.d], w=[odn.d])
            pT = psS.next()
            I("pe", "matmul", pT.t[0:64, 0:32], odn.t[0:32, 0:64], ident.t[0:32, 0:32], start=True, stop=True,
              r=[odn.d, ident.d], w=[pT.d])
            I("act", "mul", haT.t[:, :, 4 * s_:4 * s_ + 4], pT.t[0:64, 0:32].rearrange("p (h t) -> p h t", t=4), 1.0,
              r=[pT.d], w=[haT.d])
        ckpt("s_attn")
        srcs = [("w", k, hmT.t[:, k, 0:NS], hmT.d) for k in range(4)] + \
               [("a", h, haT.t[:, h, 0:NS], haT.d) for h in range(8)]
        resid_proj("w_out", xt, NS, srcs, att=True)

        def pre_mem(gi):
            DMA("pool", smk[gi % 2].t, s_mkT[gi].rearrange("(k p) m -> p k m", p=128), w=[smk[gi % 2].d],
                sem=smk[gi % 2].d.sem)
            DMA("pool", smv[gi % 2].t, s_mv[gi].rearrange("(mb p) c -> p mb c", p=128), w=[smv[gi % 2].d],
                sem=smv[gi % 2].d.sem)
        xattn(xt, NS,
              lambda gi, h, ec, mb: smk[gi % 2].t[:, 2 * h + ec, mb * 128:(mb + 1) * 128],
              lambda gi, h, ec, mb: smv[gi % 2].t[:, mb, (2 * h + ec) * 128:(2 * h + ec + 1) * 128],
              lambda gi: [smk[gi % 2].d, smv[gi % 2].d], True, pre=pre_mem)
        ffn(xt, NS)
        final_out(xt, NS, lambda k: o_ysT[k * 128:(k + 1) * 128, :])
    except _Stop:
        pass
    P.emit()
    P.run(E)
    es.close()
    return nc


_NC = None


def _get_nc():
    global _NC
    if _NC is None:
        _NC = build()
    return _NC


def make_in_maps(x_prompt, x_sample, mem_prompt, cache_attn_k, cache_attn_v, cache_mem_k, cache_mem_v,
           state_conv, state_C, state_n, state_m, g_mix, w_in, conv_w, conv_b, b_gates, g_mh, w_out,
           g_mem, w_mk, w_mv, g_xattn, w_mq, w_mo, g_ffn, w_gate, w_up, w_down, g_final, cores=None):
    f = lambda a: np.ascontiguousarray(np.asarray(a, dtype=np.float32))
    consts = host_consts()
    pos = _sel_positions()
    gvec = f(np.stack([g_mix, g_mem, g_xattn, g_ffn, g_final]))
    shared = dict(w_in=f(w_in), w_out=f(w_out), w_mk=f(w_mk), w_mv=f(w_mv), w_mq=f(w_mq), w_mo=f(w_mo),
                  w_gate=f(w_gate), w_up=f(w_up), w_down=f(w_down), gvec=gvec, conv_w=f(conv_w), conv_b=f(conv_b),
                  b_gates=f(b_gates), g_mh=f(g_mh))
    shared.update(consts)
    x_prompt = np.asarray(x_prompt, np.float32)
    in_maps = []
    for c in (range(NCORES) if cores is None else cores):
        b, j = c // 4, c % 4
        m = dict(shared)
        xb = np.zeros((4, 1024, 2048), np.float32)
        fl = np.zeros((4, 12), np.float32)
        for i in range(4):
            bi = j - 3 + i
            if bi >= 0:
                xb[i] = x_prompt[b, bi * 2048:(bi + 1) * 2048, :].T
                fl[:, i] = 1.0
                fl[:, 8 + i] = -1.0
            else:
                fl[:, 4 + i] = -1e4
        m["xblk"] = xb
        m["flags"] = fl
        m["hflag"] = np.full((128, 1), 1.0 if j >= 1 else 0.0, np.float32)
        sl = slice(16 * c, 16 * c + 16)
        m["xsT"] = f(np.asarray(x_sample)[sl].reshape(64, 1024).T)
        m["memT"] = f(np.asarray(mem_prompt)[b].T)
        ck = np.asarray(cache_attn_k)[sl][:, pos].reshape(16, NSEL, 512)
        m["s_kT"] = f(ck.transpose(0, 2, 1))
        m["s_v"] = f(np.asarray(cache_attn_v)[sl][:, pos].reshape(16, NSEL, 512))
        m["s_mkT"] = f(np.asarray(cache_mem_k)[sl].reshape(16, 256, 1024).transpose(0, 2, 1))
        m["s_mv"] = f(np.asarray(cache_mem_v)[sl].reshape(16, 256, 1024))
        m["s_convT"] = f(np.asarray(state_conv)[sl].transpose(2, 0, 1))
        m["s_C0"] = f(np.asarray(state_C)[sl])
        m["s_n0T"] = f(np.asarray(state_n)[sl].transpose(2, 0, 1))
        m["s_m0T"] = f(np.asarray(state_m)[sl].T)
        in_maps.append(m)
    return in_maps


def kernel(**inputs):
    in_maps = make_in_maps(**inputs)
    nc = _get_nc()
    res = run_bass_kernel_spmd(nc, in_maps, core_ids=list(range(NCORES)))
    return assemble(res.results)


def assemble(R):
    y_prompt = np.zeros((2, 8192, 1024), np.float32)
    for c in range(NCORES):
        b, j = c // 4, c % 4
        y_prompt[b, j * 2048:(j + 1) * 2048] = R[c]["o_yT"].T
    y_sample = np.concatenate([R[c]["o_ysT"].T.reshape(16, 4, 1024) for c in range(NCORES)], 0)
    last = [3, 7]
    p_attn_k = np.stack([R[c]["o_kaT"].T.reshape(2048, 8, 64) for c in last])
    p_attn_v = np.stack([R[c]["o_va"].reshape(2048, 8, 64) for c in last])
    p_conv = np.stack([R[c]["o_convT"].T for c in last])
    p_C = np.stack([R[c]["o_C"][:, :, 0:128] for c in last])
    p_n = np.stack([R[c]["o_C"][:, :, 128] for c in last])
    p_m = np.stack([R[c]["o_m"][:, 0] for c in last])
    p_mem_k = np.stack([R[c]["o_mkT"].T.reshape(256, 4, 256) for c in (0, 4)])
    p_mem_v = np.stack([R[c]["o_mv"].reshape(256, 4, 256) for c in (0, 4)])
    s_attn_k = np.concatenate([R[c]["o_skaT"].T.reshape(16, 4, 8, 64) for c in range(NCORES)], 0)
    s_attn_v = np.concatenate([R[c]["o_sva"].reshape(16, 4, 8, 64) for c in range(NCORES)], 0)
    s_conv = np.concatenate([R[c]["o_sconvT"].transpose(1, 2, 0) for c in range(NCORES)], 0)
    s_C = np.concatenate([R[c]["o_sC"][:, :, :, 0:128] for c in range(NCORES)], 0)
    s_n = np.concatenate([R[c]["o_sC"][:, :, :, 128] for c in range(NCORES)], 0)
    s_m = np.concatenate([R[c]["o_sm"].T for c in range(NCORES)], 0)
    outs = (y_prompt, y_sample, p_attn_k, p_attn_v, p_conv, p_C, p_n, p_m, p_mem_k, p_mem_v,
            s_attn_k, s_attn_v, s_conv, s_C, s_n, s_m)
    return tuple(np.ascontiguousarray(o, dtype=np.float32) for o in outs)
```
